# Optimizing a Trainium2 kernel written in Bass

```python
import math
import jax, jax.numpy as jnp
from jax import lax
import numpy as np

D_MODEL = 2048
BATCH = 2
SEQ = 16384
DEPTH = 1

HEAD_DIM = 128
ATTN_HEADS = 8
GMLP_GROUPS = 8
ATTN_WIDTH = ATTN_HEADS * HEAD_DIM
GMLP_WIDTH = GMLP_GROUPS * HEAD_DIM
MIX_WIDTH = ATTN_WIDTH + GMLP_WIDTH
IN_PROJ_WIDTH = 3 * ATTN_WIDTH + 2 * GMLP_WIDTH
DILATION_BRANCHES = ((128, 1), (512, 4), (2048, 16))
BAND_BLOCK = 128
GMLP_CHUNK = 128
ROPE_THETA = 10000.0
MEM_LEN = 256
XATTN_HEADS = 4
XATTN_HEAD_DIM = D_MODEL // XATTN_HEADS
D_FF = -(-(8 * D_MODEL) // (3 * 256)) * 256
DEEPNORM_ALPHA = (2 * DEPTH) ** 0.25
DEEPNORM_BETA = (8 * DEPTH) ** -0.25
LN_EPS = 1e-5

kernel_name = "hymba_dilated_gmlp_deepnorm_layer"


def _layer_norm(x, g, b):
    xf = x.astype(jnp.float32)
    mu = jnp.mean(xf, axis=-1, keepdims=True)
    var = jnp.mean(jnp.square(xf - mu), axis=-1, keepdims=True)
    y = (xf - mu) * lax.rsqrt(var + LN_EPS) * g.astype(jnp.float32) + b.astype(jnp.float32)
    return y.astype(x.dtype)


def _rms_norm(x, g):
    xf = x.astype(jnp.float32)
    y = xf * lax.rsqrt(jnp.mean(jnp.square(xf), axis=-1, keepdims=True) + LN_EPS) * g.astype(jnp.float32)
    return y.astype(x.dtype)


def _rope(x, positions):
    half = x.shape[-1] // 2
    inv_freq = ROPE_THETA ** (-jnp.arange(half, dtype=jnp.float32) / half)
    ang = positions.astype(jnp.float32)[:, :, None] * inv_freq
    cos = jnp.cos(ang)[:, :, None, :]
    sin = jnp.sin(ang)[:, :, None, :]
    xf = x.astype(jnp.float32)
    x1, x2 = xf[..., :half], xf[..., half:]
    return jnp.concatenate([x1 * cos - x2 * sin, x2 * cos + x1 * sin], axis=-1).astype(x.dtype)


def _banded_causal_attention(q, k, v, reach):
    N, L, H, E = q.shape
    nb = -(-L // BAND_BLOCK)
    Lp = nb * BAND_BLOCK
    pad = ((0, 0), (0, Lp - L), (0, 0), (0, 0))
    qb = jnp.pad(q, pad).reshape(N, nb, BAND_BLOCK, H, E)
    kb = jnp.pad(k, pad).reshape(N, nb, BAND_BLOCK, H, E)
    vb = jnp.pad(v, pad).reshape(N, nb, BAND_BLOCK, H, E)

    def with_prev(t):
        prev = jnp.pad(t[:, :-1], ((0, 0), (1, 0), (0, 0), (0, 0), (0, 0)))
        return jnp.concatenate([prev, t], axis=2)

    kw, vw = with_prev(kb), with_prev(vb)
    s = jnp.einsum('nbqhe,nbkhe->nbhqk', qb, kw).astype(jnp.float32) * (E ** -0.5)
    qpos = jnp.arange(BAND_BLOCK)[:, None] + BAND_BLOCK
    kpos = jnp.arange(2 * BAND_BLOCK)[None, :]
    dist = qpos - kpos
    band = (dist >= 0) & (dist <= reach)
    prev_ok = (jnp.arange(nb)[:, None, None] > 0) | (kpos[None] >= BAND_BLOCK)
    mask = band[None] & prev_ok
    s = jnp.where(mask[None, :, None], s, -jnp.inf)
    lse = jax.nn.logsumexp(s, axis=-1)
    p = jnp.exp(s - lse[..., None])
    o = jnp.einsum('nbhqk,nbkhe->nbqhe', p.astype(v.dtype), vw)
    o = o.reshape(N, Lp, H, E)[:, :L]
    lse = lse.transpose(0, 1, 3, 2).reshape(N, Lp, H)[:, :L]
    return o, lse


def _dilated_attention(q, k, v):
    B, S, H, E = q.shape
    outs, lses = [], []
    for window, dil in DILATION_BRANCHES:
        reach = window // dil
        Ld = S // dil

        def to_strided(t):
            return t.reshape(B, Ld, dil, H, E).transpose(0, 2, 1, 3, 4).reshape(B * dil, Ld, H, E)

        o, lse = _banded_causal_attention(to_strided(q), to_strided(k), to_strided(v), reach)
        outs.append(o.reshape(B, dil, Ld, H, E).transpose(0, 2, 1, 3, 4).reshape(B, S, H, E))
        lses.append(lse.reshape(B, dil, Ld, H).transpose(0, 2, 1, 3).reshape(B, S, H))
    wts = jax.nn.softmax(jnp.stack(lses, axis=0), axis=0)
    out = jnp.einsum('rbsh,rbshe->bshe', wts, jnp.stack(outs, axis=0).astype(jnp.float32))
    return out.astype(q.dtype)


def _spatial_gating(u, g, norm_g, norm_b, w_s, b_s):
    B, S, _ = u.shape
    u = jax.nn.gelu(u, approximate=False)
    g = _layer_norm(jax.nn.gelu(g, approximate=False), norm_g, norm_b)
    gc = g.reshape(B, S // GMLP_CHUNK, GMLP_CHUNK, GMLP_GROUPS, HEAD_DIM)
    w = jnp.tril(w_s).astype(g.dtype)
    mixed = jnp.einsum('gij,bcjge->bcige', w, gc) + b_s.T[None, None, :, :, None]
    return u * mixed.reshape(B, S, GMLP_WIDTH)


def _memory_cross_attention(h, mem, w_q, w_k, w_v, w_o):
    B, S, D = h.shape
    M = mem.shape[1]
    q = (h @ w_q).reshape(B, S, XATTN_HEADS, XATTN_HEAD_DIM)
    k = (mem @ w_k).reshape(B, M, XATTN_HEADS, XATTN_HEAD_DIM)
    v = (mem @ w_v).reshape(B, M, XATTN_HEADS, XATTN_HEAD_DIM)
    s = jnp.einsum('bshe,bmhe->bhsm', q, k).astype(jnp.float32) * (XATTN_HEAD_DIM ** -0.5)
    p = jax.nn.softmax(s, axis=-1)
    o = jnp.einsum('bhsm,bmhe->bshe', p.astype(v.dtype), v).reshape(B, S, D)
    return o @ w_o


def setup_inputs(seed: int = 0) -> dict:
    key = jax.random.key(seed)
    ks = jax.random.split(key, 32)
    f32 = jnp.float32
    L = DEPTH
    D = D_MODEL

    def nrm(k, shape, scale):
        return jax.random.normal(k, shape, f32) * scale

    def gain(k, shape):
        return 1.0 + 0.02 * jax.random.normal(k, shape, f32)

    def bias(k, shape):
        return 0.02 * jax.random.normal(k, shape, f32)

    x = jax.random.normal(ks[0], (BATCH, SEQ, D), f32)
    mem = jax.random.normal(ks[1], (BATCH, MEM_LEN, D), f32)
    start = jax.random.randint(ks[2], (BATCH, 1), 0, 4096, dtype=jnp.int32)
    positions = start + jnp.arange(SEQ, dtype=jnp.int32)[None, :]

    col_scale = jnp.concatenate([jnp.ones((2 * ATTN_WIDTH,), f32),
                                 jnp.full((ATTN_WIDTH + 2 * GMLP_WIDTH,), DEEPNORM_BETA, f32)])
    w_in = nrm(ks[5], (L, D, IN_PROJ_WIDTH), D ** -0.5) * col_scale

    return {
        "x": x,
        "mem": mem,
        "positions": positions,
        "ln_in_g": gain(ks[3], (D,)),
        "ln_in_b": bias(ks[4], (D,)),
        "w_in": w_in,
        "sgu_norm_g": gain(ks[6], (L, GMLP_WIDTH)),
        "sgu_norm_b": bias(ks[7], (L, GMLP_WIDTH)),
        "w_spatial": nrm(ks[8], (L, GMLP_GROUPS, GMLP_CHUNK, GMLP_CHUNK), GMLP_CHUNK ** -0.5),
        "b_spatial": 1.0 + 0.1 * jax.random.normal(ks[9], (L, GMLP_GROUPS, GMLP_CHUNK), f32),
        "attn_out_g": gain(ks[10], (L, ATTN_WIDTH)),
        "gmlp_out_g": gain(ks[11], (L, GMLP_WIDTH)),
        "w_mix_out": nrm(ks[12], (L, MIX_WIDTH, D), DEEPNORM_BETA * MIX_WIDTH ** -0.5),
        "ln1_g": gain(ks[13], (L, D)),
        "ln1_b": bias(ks[14], (L, D)),
        "w_xq": nrm(ks[15], (L, D, D), D ** -0.5),
        "w_xk": nrm(ks[16], (L, D, D), D ** -0.5),
        "w_xv": nrm(ks[17], (L, D, D), DEEPNORM_BETA * D ** -0.5),
        "w_xo": nrm(ks[18], (L, D, D), DEEPNORM_BETA * D ** -0.5),
        "ln2_g": gain(ks[19], (L, D)),
        "ln2_b": bias(ks[20], (L, D)),
        "w_ffn_gate": nrm(ks[21], (L, D, D_FF), DEEPNORM_BETA * D ** -0.5),
        "w_ffn_up": nrm(ks[22], (L, D, D_FF), DEEPNORM_BETA * D ** -0.5),
        "w_ffn_down": nrm(ks[23], (L, D_FF, D), DEEPNORM_BETA * D_FF ** -0.5),
        "ln3_g": gain(ks[24], (L, D)),
        "ln3_b": bias(ks[25], (L, D)),
    }


def reference(x, mem, positions, ln_in_g, ln_in_b, w_in, sgu_norm_g, sgu_norm_b,
              w_spatial, b_spatial, attn_out_g, gmlp_out_g, w_mix_out, ln1_g, ln1_b,
              w_xq, w_xk, w_xv, w_xo, ln2_g, ln2_b, w_ffn_gate, w_ffn_up, w_ffn_down,
              ln3_g, ln3_b):
    B, S, D = x.shape
    h = _layer_norm(x, ln_in_g, ln_in_b)
    for l in range(DEPTH):
        proj = h @ w_in[l]
        q, k, v, u, g = jnp.split(
            proj, [ATTN_WIDTH, 2 * ATTN_WIDTH, 3 * ATTN_WIDTH, 3 * ATTN_WIDTH + GMLP_WIDTH], axis=-1)
        q = _rope(q.reshape(B, S, ATTN_HEADS, HEAD_DIM), positions)
        k = _rope(k.reshape(B, S, ATTN_HEADS, HEAD_DIM), positions)
        v = v.reshape(B, S, ATTN_HEADS, HEAD_DIM)
        attn = _dilated_attention(q, k, v).reshape(B, S, ATTN_WIDTH)
        gm = _spatial_gating(u, g, sgu_norm_g[l], sgu_norm_b[l], w_spatial[l], b_spatial[l])
        mixed = jnp.concatenate([_rms_norm(attn, attn_out_g[l]),
                                 _rms_norm(gm, gmlp_out_g[l])], axis=-1) @ w_mix_out[l]
        h = _layer_norm(DEEPNORM_ALPHA * h + mixed, ln1_g[l], ln1_b[l])
        xa = _memory_cross_attention(h, mem, w_xq[l], w_xk[l], w_xv[l], w_xo[l])
        h = _layer_norm(DEEPNORM_ALPHA * h + xa, ln2_g[l], ln2_b[l])
        ff = (jax.nn.silu(h @ w_ffn_gate[l]) * (h @ w_ffn_up[l])) @ w_ffn_down[l]
        h = _layer_norm(DEEPNORM_ALPHA * h + ff, ln3_g[l], ln3_b[l])
    return h
```

```python
import contextlib
import numpy as np
import concourse.bass as bass
import concourse.mybir as mybir
from concourse.bass_utils import run_bass_kernel_spmd

F32 = mybir.dt.float32
BF16 = mybir.dt.bfloat16
I32 = mybir.dt.int32
AF = mybir.ActivationFunctionType
ALU = mybir.AluOpType

D = 2048
NH = 8
TOWN = 4096
THALO = 2048
TTOT = TOWN + THALO
TT = 512
DFF = 5632
MEM = 256
ALPHA = float(2.0 ** 0.25)
EPS = 1e-5
VW = NH * 129
PI = float(np.float32(np.pi))
HALF_PI = float(np.float32(np.pi / 2))
INV_2PI = float(np.float32(1.0 / (2 * np.pi)))
CW1 = 6.28125
CW2 = float(np.float32(2 * np.pi - 6.28125))
BRANCH_D = (1, 4, 16)

WSPEC = [("w_in", 2048, 5120), ("w_mix", 2048, 2048), ("w_xq", 2048, 2048),
         ("w_xo", 2048, 2048), ("w_gate", 2048, DFF), ("w_up", 2048, DFF),
         ("w_down", DFF, 2048)]
DOWN_KSPLIT = [(0, 12), (12, 12), (24, 10), (34, 10)]


def _unit_table():
    units = {}
    uid = 0
    for name, K, N in WSPEC:
        ks = DOWN_KSPLIT if name == "w_down" else [(0, K // 128)]
        tab = []
        for cg in range(N // 512):
            row = []
            for (k0, nkc) in ks:
                row.append((uid, k0, nkc))
                uid += 1
            tab.append(row)
        units[name] = tab
    return units, uid


UNITS, NUNITS = _unit_table()
DBG = {}
STQ = "act"


class Sched:
    def __init__(self, nc, es):
        self.nc = nc
        self.eng = {"pe": nc.tensor, "act": nc.scalar, "dve": nc.vector, "pool": nc.gpsimd,
                    "sp": nc.sync}
        self.sem = {e: es.enter_context(nc.semaphore("sem_" + e)) for e in ("pe", "act", "dve", "pool")}
        self.cnt = {e: 0 for e in self.sem}
        self.seen = {e: {} for e in self.eng}
        self.lastw = {}
        self.readers = {}
        nd = {"sp": 28, "pool": 10, "act": 6}
        self.dsem = {q: [es.enter_context(nc.semaphore("d_%s_%d" % (q, i))) for i in range(n)]
                     for q, n in nd.items()}
        self.dcnt = {q: [0] * n for q, n in nd.items()}
        self.dnext = {q: 0 for q in nd}
        self.ninst = 0

    def _need(self, eng, tok):
        kind, ident, n = tok
        if kind == "c" and ident == eng:
            return
        key = (kind, ident)
        if self.seen[eng].get(key, 0) >= n:
            return
        self.seen[eng][key] = n
        sem = self.sem[ident] if kind == "c" else self.dsem[ident[0]][ident[1]]
        self.eng[eng].wait_ge(sem, n)
        self.ninst += 1

    def _need_same(self, eng, tok):
        kind, ident, n = tok
        key = (kind, ident)
        if self.seen[eng].get(key, 0) >= n:
            return
        self.seen[eng][key] = n
        self.eng[eng].wait_ge(self.sem[ident], n)
        self.ninst += 1

    def _deps(self, eng, reads, writes):
        for k in reads:
            w = self.lastw.get(k)
            if w is not None:
                if w[0] == "c" and w[1] == eng:
                    if eng != "pe":
                        self._need_same(eng, w)
                else:
                    self._need(eng, w)
        for k in writes:
            w = self.lastw.get(k)
            if w is not None:
                self._need(eng, w)
            for (kind, ident), n in self.readers.get(k, {}).items():
                self._need(eng, (kind, ident, n))

    def _record(self, tok, reads, writes):
        key = (tok[0], tok[1])
        for k in reads:
            self.readers.setdefault(k, {})[key] = tok[2]
        for k in writes:
            self.lastw[k] = tok
            self.readers[k] = {}

    def op(self, eng, emit, reads=(), writes=()):
        self._deps(eng, reads, writes)
        inst = emit(self.eng[eng])
        inst.then_inc(self.sem[eng], 1)
        self.cnt[eng] += 1
        self.ninst += 1
        tok = ("c", eng, self.cnt[eng])
        self._record(tok, reads, writes)
        return tok

    def dma(self, q, out, in_, reads=(), writes=()):
        self._deps(q, reads, writes)
        i = self.dnext[q]
        self.dnext[q] = (i + 1) % len(self.dsem[q])
        if self.dcnt[q][i] > 0:
            self._need(q, ("d", (q, i), self.dcnt[q][i]))
        self.eng[q].dma_start(out=out, in_=in_).then_inc(self.dsem[q][i], 16)
        self.dcnt[q][i] += 16
        self.ninst += 1
        tok = ("d", (q, i), self.dcnt[q][i])
        self._record(tok, reads, writes)
        return tok

    def barrier(self, engines=("pe", "act", "dve", "pool", "sp")):
        for e in engines:
            for o in self.sem:
                if self.cnt[o] > 0:
                    if o == e:
                        continue
                    self._need(e, ("c", o, self.cnt[o]))
            for q in self.dsem:
                for i, n in enumerate(self.dcnt[q]):
                    if n > 0:
                        self._need(e, ("d", (q, i), n))
        self.lastw = {}
        self.readers = {}


class _Stop(Exception):
    pass


def build_program(debug=False, stop_after=None):
    nc = bass.Bass("TRN2", target_bir_lowering=False)

    def din(name, shape, dt=F32):
        return nc.dram_tensor(name, list(shape), dt, kind="ExternalInput").ap()

    x = din("x", [TTOT, D])
    mem = din("mem", [MEM, D])
    pos = din("pos", [128, TTOT], I32)
    hv = din("hv", [128, 1])
    invf = din("invf", [128, 1])
    tabs = {}
    for nm in ("ln_in_g", "ln_in_b", "ln1_g", "ln1_b", "ln2_g", "ln2_b", "ln3_g", "ln3_b"):
        tabs[nm] = din(nm, [128, D])
    for nm in ("sgu_norm_g", "sgu_norm_b", "attn_out_g", "gmlp_out_g"):
        tabs[nm] = din(nm, [128, 1024])
    w_sp = din("w_spatial", [8, 128, 128])
    b_sp = din("b_spatial", [8, 128])
    wd = {}
    for name, K, N in WSPEC:
        wd[name] = din(name, [K, N])
    wd["w_xk"] = din("w_xk", [D, D])
    wd["w_xv"] = din("w_xv", [D, D])
    out = nc.dram_tensor("out", [TOWN, D], F32, kind="ExternalOutput").ap()

    skind = "ExternalOutput" if debug else "Internal"

    def dscr(name, shape, dt):
        return nc.dram_tensor(name, list(shape), dt, kind=skind).ap()

    wscr = dscr("wscr", [NUNITS, 128, 8192], BF16)
    qT_scr = dscr("qT_scr", [NH, 128, TOWN], BF16)
    kT_scr = dscr("kT_scr", [NH, 128, TTOT], BF16)
    v_scr = dscr("v_scr", [TTOT, VW], BF16)
    gmT_scr = dscr("gmT_scr", [128, 8, TOWN], BF16)
    attn_scr = dscr("attn_scr", [3, TOWN, VW], F32)
    attnT_scr = dscr("attnT_scr", [128, 8, TOWN], BF16)
    KmemT_scr = dscr("KmemT_scr", [128, 16 * MEM], BF16)
    h_scr = nc.dram_tensor("h_scr", [TOWN, D], F32).ap()
    Vmem_scr = dscr("Vmem_scr", [128, 2 * D], BF16)

    with contextlib.ExitStack() as es:
        S = Sched(nc, es)

        def sb(stack, name, shape, dt=F32):
            return stack.enter_context(nc.sbuf_tensor(name, list(shape), dt))

        ps = [es.enter_context(nc.psum_tensor("ps%d" % b, [128, 512], F32)) for b in range(8)]
        psbf = [p[:].bitcast(BF16) for p in ps]

        ident = sb(es, "ident", [128, 128], BF16)
        identf = sb(es, "identf", [128, 128])
        onesbf = sb(es, "onesbf", [128, 128], BF16)
        mask2 = sb(es, "mask2", [128, 256], BF16)
        maskf = sb(es, "maskf", [128, 256])
        eps_c = sb(es, "eps_c", [128, 1])
        hpi_c = sb(es, "hpi_c", [128, 1])
        sgn_c = sb(es, "sgn_c", [128, 1])
        one_c = sb(es, "one_c", [128, 1])
        hv_c = sb(es, "hv_c", [128, 1])
        invf_c = sb(es, "invf_c", [128, 1])
        wstate = {"n": 0, "wr": None}
        accs = {"n": 0, "nb": 4}

        def acc_bank():
            b = accs["n"] % accs["nb"]
            accs["n"] += 1
            return b

        def wload(unit):
            uid, k0, nkc = unit
            if uid in wstate.setdefault("pref", {}):
                return wstate["pref"].pop(uid)
            return wfetch(unit)

        def wprefetch(unit):
            wstate.setdefault("pref", {})[unit[0]] = wfetch(unit)

        def wfetch(unit):
            uid, k0, nkc = unit
            wr = wstate["wr"]
            slot = wstate["n"] % len(wr)
            wstate["n"] += 1
            S.dma("sp", wr[slot][:, 0:nkc, :],
                  wscr[uid, :, 0:nkc * 512].rearrange("p (k c) -> p k c", c=512),
                  reads=[("wscr", uid)], writes=[("wr", slot)])
            return slot

        def c_consts(e):
            e.memset(eps_c[:], EPS)
            e.memset(hpi_c[:], HALF_PI)
            e.memset(one_c[:], 1.0)
            e.memset(sgn_c[0:64, :], -1.0)
            e.memset(sgn_c[64:128, :], 1.0)
            e.memset(identf[:], 0.0)
            return e.memset(maskf[:], 1.0)
        S.op("pool", c_consts, writes=["consts", "identf", "maskf"])
        S.op("pool", lambda e: e.affine_select(out=identf[:], in_=identf[:], pattern=[[-1, 128]],
                                               compare_op=ALU.not_equal, fill=1.0, base=0,
                                               channel_multiplier=1),
             reads=["identf"], writes=["identf"])

        def c_masks(e):
            e.affine_select(out=maskf[:, 0:128], in_=maskf[:, 0:128], pattern=[[-1, 128]],
                            compare_op=ALU.is_ge, fill=0.0, base=0, channel_multiplier=1)
            return e.affine_select(out=maskf[:, 128:256], in_=maskf[:, 128:256], pattern=[[1, 128]],
                                   compare_op=ALU.is_ge, fill=0.0, base=0, channel_multiplier=-1)
        S.op("pool", c_masks, reads=["maskf"], writes=["maskf"])
        S.op("dve", lambda e: e.tensor_copy(ident[:], identf[:]), reads=["identf"], writes=["ident"])
        S.op("dve", lambda e: e.tensor_copy(mask2[:], maskf[:]), reads=["maskf"], writes=["mask2"])
        S.op("pool", lambda e: e.memset(onesbf[:], 1.0), writes=["onesbf"])
        S.dma("sp", hv_c[:], hv, writes=["hv_c"])
        S.dma("sp", invf_c[:], invf, writes=["invf_c"])

        def transposes(src_fn, n, banks, rkeys):
            def emit(e):
                last = None
                for i in range(n):
                    b = banks[i // 8]
                    last = e.transpose(psbf[b][:, (i % 8) * 128:(i % 8 + 1) * 128], src_fn(i), ident[:])
                return last
            S.op("pe", emit, reads=list(rkeys) + ["ident"], writes=[("ps", b) for b in banks[:(n + 7) // 8]])

        def layer_norm_multi(stk, items):
            bnst, mv, sd, rstd, nmr = stk
            for i, it in enumerate(items):
                xin = it["xin"]
                nchk = xin.shape[-1] // 512
                def e_stats(e, xin=xin, nchk=nchk, i=i):
                    last = None
                    for c in range(nchk):
                        last = e.bn_stats(bnst[:, i, c, :], xin[:, c * 512:(c + 1) * 512])
                    return last
                S.op("dve", e_stats, reads=[it["xkey"]], writes=[("bnst", i)])
            for i, it in enumerate(items):
                nchk = it["xin"].shape[-1] // 512
                S.op("dve", lambda e, i=i, nchk=nchk: e.bn_aggr(
                    mv[:, i, :], bnst[:, i, 0:nchk, :].rearrange("p a b -> p (a b)")),
                    reads=[("bnst", i)], writes=[("mv", i)])
            for i, it in enumerate(items):
                S.op("act", lambda e, i=i: e.activation(out=sd[:, i:i + 1], in_=mv[:, i, 1:2], func=AF.Sqrt,
                                                        bias=eps_c[:], scale=1.0),
                     reads=[("mv", i), "consts"], writes=[("sd", i)])
            for i, it in enumerate(items):
                S.op("dve", lambda e, i=i: e.reciprocal(rstd[:, i:i + 1], sd[:, i:i + 1]),
                     reads=[("sd", i)], writes=[("rstd", i)])
                S.op("dve", lambda e, i=i: e.tensor_scalar(nmr[:, i:i + 1], mv[:, i, 0:1], -1.0, rstd[:, i:i + 1],
                                                           ALU.mult, ALU.mult),
                     reads=[("mv", i), ("rstd", i)], writes=[("nmr", i)])
            for i, it in enumerate(items):
                xin = it["xin"]
                S.op("act", lambda e, i=i, xin=xin: e.activation(out=xin, in_=xin, func=AF.Identity,
                                                                 bias=nmr[:, i:i + 1], scale=rstd[:, i:i + 1]),
                     reads=[it["xkey"], ("rstd", i), ("nmr", i)], writes=[it["xkey"]])
            for i, it in enumerate(items):
                xin = it["xin"]
                wh = (xin.shape[-1] * 5) // 8
                wh = xin.shape[-1] // 2
                it["hkeys"] = [(it["xkey"], "lo"), (it["xkey"], "hi")]
                S.op("pool", lambda e, xin=xin, it=it, wh=wh: e.tensor_tensor(
                    xin[:, 0:wh], xin[:, 0:wh], it["gt"][:, 0:wh], ALU.mult),
                    reads=[it["xkey"]] + it["tkeys"], writes=[it["hkeys"][0]])
                S.op("dve", lambda e, xin=xin, it=it, wh=wh: e.tensor_tensor(
                    xin[:, wh:], xin[:, wh:], it["gt"][:, wh:], ALU.mult),
                    reads=[it["xkey"]] + it["tkeys"], writes=[it["hkeys"][1]])
            for i, it in enumerate(items):
                xin = it["xin"]
                if it.get("out_f32") is not None:
                    S.op("dve", lambda e, xin=xin, it=it: e.tensor_tensor(it["out_f32"], xin, it["bt"], ALU.add),
                         reads=[it["xkey"]] + it["hkeys"] + it["tkeys"], writes=[it["okey_f"]])
                elif it.get("out_bf") is not None:
                    S.op("dve", lambda e, xin=xin, it=it: e.tensor_tensor(it["out_bf"], xin, it["bt"], ALU.add),
                         reads=[it["xkey"]] + it["hkeys"] + it["tkeys"], writes=[it["okey_b"]])
            for i, it in enumerate(items):
                if it.get("out_f32") is not None and it.get("out_bf") is not None:
                    S.op("act", lambda e, it=it: e.activation(out=it["out_bf"], in_=it["out_f32"], func=AF.Copy),
                         reads=[it["okey_f"]], writes=[it["okey_b"]])

        def layer_norm(stk, xin, xkey, gt, bt, tkeys, out_f32=None, okey_f=None, out_bf=None, okey_b=None):
            layer_norm_multi(stk, [dict(xin=xin, xkey=xkey, gt=gt, bt=bt, tkeys=tkeys, out_f32=out_f32,
                                        okey_f=okey_f, out_bf=out_bf, okey_b=okey_b)])

        def mm_group(bank_ap, pairs, rkeys, bank):
            n = len(pairs)
            def emit(e):
                last = None
                for i, (l, r) in enumerate(pairs):
                    last = e.matmul(bank_ap, l, r, start=(i == 0), stop=(i == n - 1))
                return last
            S.op("pe", emit, reads=rkeys, writes=[("ps", bank)])

        cjobs = []
        for name, K, N in WSPEC[1:]:
            for cg, row in enumerate(UNITS[name]):
                for (uid, k0, nkc) in row:
                    h1 = (nkc + 1) // 2
                    cjobs.append((name, uid, cg, k0, h1, 0))
                    cjobs.append((name, uid, cg, k0 + h1, nkc - h1, h1))
        cpos = {"next": 0}

        def make_conv(cstL, cbfL, tag, engines):
            n = len(cstL)
            st = {"loaded": [], "cast": [], "k": 0, "e": 0}

            def load_one():
                if cpos["next"] >= len(cjobs):
                    return
                job = cjobs[cpos["next"]]
                cpos["next"] += 1
                slot = st["k"] % n
                st["k"] += 1
                name, uid, cg, kr, nk, koff = job
                src = wd[name][kr * 128:(kr + nk) * 128, cg * 512:(cg + 1) * 512]
                S.dma("sp", cstL[slot][:, 0:nk, :], src.rearrange("(k p) c -> p k c", p=128),
                      writes=[(tag + "cst", slot)])
                st["loaded"].append((job, slot))

            def cast_one():
                if not st["loaded"]:
                    return
                job, slot = st["loaded"].pop(0)
                nk = job[4]
                eng = engines[st["e"] % len(engines)]
                st["e"] += 1
                if eng == "act":
                    S.op("act", lambda e: e.activation(out=cbfL[slot][:, 0:nk, :], in_=cstL[slot][:, 0:nk, :],
                                                       func=AF.Copy),
                         reads=[(tag + "cst", slot)], writes=[(tag + "cbf", slot)])
                else:
                    S.op(eng, lambda e: e.tensor_copy(cbfL[slot][:, 0:nk, :], cstL[slot][:, 0:nk, :]),
                         reads=[(tag + "cst", slot)], writes=[(tag + "cbf", slot)])
                st["cast"].append((job, slot))

            def store_one():
                if not st["cast"]:
                    return
                job, slot = st["cast"].pop(0)
                name, uid, cg, kr, nk, koff = job
                S.dma(STQ, wscr[uid, :, koff * 512:(koff + nk) * 512].rearrange("p (k c) -> p k c", c=512),
                      cbfL[slot][:, 0:nk, :], reads=[(tag + "cbf", slot)], writes=[("wscr", uid, koff)])

            def step():
                store_one()
                cast_one()
                load_one()

            def drain():
                while st["loaded"] or st["cast"]:
                    store_one()
                    cast_one()

            return step, drain, load_one

        try:
            with contextlib.ExitStack() as p0:
                cst = [sb(p0, "cst%d" % i, [128, 16, 512]) for i in range(3)]
                cbf = [sb(p0, "cbf%d" % i, [128, 16, 512], BF16) for i in range(3)]
                memt = sb(p0, "memt", [128, 2, D])
                membf = sb(p0, "membf", [128, 2, D], BF16)
                memT = sb(p0, "memT", [128, 16, MEM], BF16)
                KmemT = sb(p0, "KmemT0", [128, 16, MEM], BF16)
                Vmem = sb(p0, "Vmem0", [128, 2, D], BF16)
                cstate = {"n": 0}

                def convert(wname, k0, nkc, cg):
                    i = cstate["n"]
                    cstate["n"] += 1
                    slot = i % 3
                    src = wd[wname][k0 * 128:(k0 + nkc) * 128, cg * 512:(cg + 1) * 512]
                    S.dma("sp", cst[slot][:, 0:nkc, :], src.rearrange("(k p) c -> p k c", p=128),
                          writes=[("cst", slot)])
                    eng = ("act", "dve", "pool")[i % 3]
                    if eng == "act":
                        S.op("act", lambda e: e.activation(out=cbf[slot][:, 0:nkc, :], in_=cst[slot][:, 0:nkc, :],
                                                           func=AF.Copy),
                             reads=[("cst", slot)], writes=[("cbf", slot)])
                    else:
                        S.op(eng, lambda e: e.tensor_copy(cbf[slot][:, 0:nkc, :], cst[slot][:, 0:nkc, :]),
                             reads=[("cst", slot)], writes=[("cbf", slot)])
                    return slot

                S.dma("sp", memt[:], mem.rearrange("(t p) d -> p t d", p=128), writes=["memt"])
                S.op("dve", lambda e: e.tensor_copy(membf[:], memt[:]), reads=["memt"], writes=["membf"])
                for t in range(2):
                    transposes(lambda i, t=t: membf[:, t, i * 128:(i + 1) * 128], 16, [6, 7], ["membf"])
                    S.op("act", lambda e, t=t: e.activation(
                        out=memT[:, 0:8, t * 128:(t + 1) * 128],
                        in_=psbf[6].rearrange("p (k c) -> p k c", c=128), func=AF.Copy),
                        reads=[("ps", 6)], writes=["memT"])
                    S.op("dve", lambda e, t=t: e.tensor_copy(
                        memT[:, 8:16, t * 128:(t + 1) * 128],
                        psbf[7].rearrange("p (k c) -> p k c", c=128)),
                        reads=[("ps", 7)], writes=["memT"])
                for cg in range(4):
                    slot = convert("w_xk", 0, 16, cg)
                    for ec in range(4):
                        b = acc_bank()
                        mm_group(ps[b][:, 0:MEM],
                                 [(cbf[slot][:, kc, ec * 128:(ec + 1) * 128], memT[:, kc, :]) for kc in range(16)],
                                 [("cbf", slot), "memT"], b)
                        S.op("act", lambda e, b=b, cg=cg, ec=ec: e.activation(
                            out=KmemT[:, cg * 4 + ec, :], in_=ps[b][:, 0:MEM], func=AF.Copy),
                            reads=[("ps", b)], writes=["KmemT"])
                for cg in range(4):
                    slot = convert("w_xv", 0, 16, cg)
                    for t in range(2):
                        b = acc_bank()
                        mm_group(ps[b][:],
                                 [(memT[:, kc, t * 128:(t + 1) * 128], cbf[slot][:, kc, :]) for kc in range(16)],
                                 [("cbf", slot), "memT"], b)
                        S.op("dve", lambda e, b=b, cg=cg, t=t: e.tensor_copy(
                            Vmem[:, t, cg * 512:(cg + 1) * 512], ps[b][:]),
                            reads=[("ps", b)], writes=["Vmem"])
                S.dma("sp", KmemT_scr, KmemT[:].rearrange("p k m -> p (k m)"), reads=["KmemT"], writes=["KmemT_scr"])
                S.dma("sp", Vmem_scr, Vmem[:].rearrange("p t d -> p (t d)"), reads=["Vmem"], writes=["Vmem_scr"])
                for name, K, N in WSPEC[:1]:
                    for cg, row in enumerate(UNITS[name]):
                        for (uid, k0, nkc) in row:
                            slot = convert(name, k0, nkc, cg)
                            S.dma("sp", wscr[uid, :, 0:nkc * 512].rearrange("p (k c) -> p k c", c=512),
                                  cbf[slot][:, 0:nkc, :], reads=[("cbf", slot)], writes=[("wscr", uid)])
                S.barrier()
            if stop_after == '0':
                raise _Stop

            with contextlib.ExitStack() as pa:
              if 'A' not in DBG.get('skip', []):
                  lnG = sb(pa, "lnG", [128, D])
                  lnB = sb(pa, "lnB", [128, D])
                  sguG = sb(pa, "sguG", [128, 1024])
                  sguB = sb(pa, "sguB", [128, 1024])
                  gmG = sb(pa, "gmG", [128, 1024])
                  WsT = sb(pa, "WsT", [128, 8, 128], BF16)
                  bsT = sb(pa, "bsT", [128, 8])
                  xt = [sb(pa, "xt%d" % i, [128, D]) for i in range(4)]
                  hbf = [sb(pa, "hbf%d" % i, [128, D], BF16) for i in range(4)]
                  hT = sb(pa, "hT", [128, 16, TT], BF16)
                  posi = sb(pa, "posi", [128, TT], I32)
                  posf = sb(pa, "posf", [128, TT])
                  kf = sb(pa, "kf", [128, TT])
                  ki = posi
                  ang = sb(pa, "ang", [128, TT])
                  rabs = kf
                  sinS = sb(pa, "sinS", [128, TT])
                  cosT = sb(pa, "cosT", [128, TT])
                  xs = [sb(pa, "xs%d" % i, [128, TT]) for i in range(2)]
                  t1 = [sb(pa, "t1_%d" % i, [128, TT]) for i in range(2)]
                  qst = [sb(pa, "qst%d" % i, [128, TT], BF16) for i in range(3)]
                  vst = [sb(pa, "vst%d" % i, [128, 4, 129], BF16) for i in range(3)]
                  ugl = sb(pa, "ugl", [128, 4, 1024])
                  ggl = sb(pa, "ggl", [128, 4, 1024])
                  gln = sb(pa, "gln", [128, 1024], BF16)
                  gtmp = sb(pa, "gtmp", [128, 1024])
                  gm = gtmp
                  gmn = sb(pa, "gmn", [128, 1024], BF16)
                  gjunk = gmn
                  gmTs = [sb(pa, "gmTs%d" % i, [128, 8, 128], BF16) for i in range(2)]
                  lnstk = (sb(pa, "bnstA", [128, 4, 4, 6]), sb(pa, "mvA", [128, 4, 2]), sb(pa, "sdA", [128, 4]),
                           sb(pa, "rstdA", [128, 4]), sb(pa, "nmrA", [128, 4]))
                  ssq = sb(pa, "ssq", [128, 1])
                  sd2 = sb(pa, "sd2", [128, 1])
                  rs2 = sb(pa, "rs2", [128, 1])
                  accs["nb"] = 4
                  wr = [sb(pa, "wrA%d" % i, [128, 16, 512], BF16) for i in range(2)]
                  wstate["wr"] = wr
                  wstate["n"] = 0

                  S.dma("sp", lnG[:], tabs["ln_in_g"], writes=["tabA"])
                  S.dma("sp", lnB[:], tabs["ln_in_b"], writes=["tabA"])
                  S.dma("sp", sguG[:], tabs["sgu_norm_g"], writes=["tabA"])
                  S.dma("sp", sguB[:], tabs["sgu_norm_b"], writes=["tabA"])
                  S.dma("sp", gmG[:], tabs["gmlp_out_g"], writes=["tabA"])
                  WsN = ugl[:, 0, :].rearrange("p (g j) -> p g j", j=128)
                  WsNb = gln[:].rearrange("p (g j) -> p g j", j=128)
                  S.dma("sp", WsN, w_sp.rearrange("g i j -> i g j"), writes=["WsN", ("ugl", 0)])
                  with nc.allow_non_contiguous_dma(reason="tiny b_spatial transpose"):
                      S.dma("sp", bsT[:], b_sp.rearrange("g i -> i g"), writes=["bsT"])
                  S.op("dve", lambda e: e.tensor_tensor(
                      WsNb, WsN, maskf[:, 0:128].unsqueeze(1).to_broadcast([128, 8, 128]), ALU.mult),
                      reads=["WsN", "maskf", ("ugl", 0)], writes=["WsNb", ("gln", 0)])
                  transposes(lambda i: WsNb[:, i, :], 8, [6], ["WsNb", ("gln", 0)])
                  S.op("act", lambda e: e.activation(out=WsT[:], in_=psbf[6].rearrange("p (k c) -> p k c", c=128),
                                                     func=AF.Copy), reads=[("ps", 6)], writes=["WsT"])

                  gln2 = [gln, sb(pa, "gln_b", [128, 1024], BF16)]
                  gmn2 = [gmn, sb(pa, "gmn_b", [128, 1024], BF16)]
                  pend = {"t": None}

                  def gm_a(j):
                      layer_norm(lnstk, ggl[:, j, :], ("ggl", j), sguG[:], sguB[:], ["tabA"],
                                 out_bf=gln2[j % 2][:], okey_b=("gln", j % 2))

                  def gm_b(j):
                      gl = gln2[j % 2]
                      gn = gmn2[j % 2]
                      def e_sp(e):
                          last = None
                          for g in range(8):
                              last = e.matmul(ps[4 + g // 4][:, (g % 4) * 128:(g % 4 + 1) * 128],
                                              WsT[:, g, :], gl[:, g * 128:(g + 1) * 128], start=True, stop=True)
                          return last
                      S.op("pe", e_sp, reads=["WsT", ("gln", j % 2)], writes=[("ps", 4), ("ps", 5)])
                      for hb in range(2):
                          S.op("dve", lambda e, hb=hb: e.tensor_tensor(
                              gtmp[:, hb * 512:(hb + 1) * 512].rearrange("p (g e) -> p g e", e=128),
                              ps[4 + hb][:].rearrange("p (g e) -> p g e", e=128),
                              bsT[:, hb * 4:(hb + 1) * 4].unsqueeze(2).to_broadcast([128, 4, 128]), ALU.add),
                              reads=[("ps", 4 + hb), "bsT"], writes=[("gtmp", hb)])
                      S.op("pool", lambda e: e.tensor_tensor(gm[:], gtmp[:], ugl[:, j, :], ALU.mult),
                           reads=[("gtmp", 0), ("gtmp", 1), ("ugl", j)], writes=["gm", ("gtmp", 0), ("gtmp", 1)])
                      S.op("pool", lambda e: e.memset(ssq[:], 0.0), writes=["ssq"])
                      S.op("act", lambda e: e.activation(out=gn[:], in_=gm[:], func=AF.Square, accum_out=ssq[:]),
                           reads=["gm", "ssq"], writes=["ssq", ("gmn", j % 2)])
                      S.op("act", lambda e: e.activation(out=sd2[:], in_=ssq[:], func=AF.Sqrt, bias=eps_c[:],
                                                         scale=1.0 / 1024),
                           reads=["ssq", "consts"], writes=["sd2"])
                      S.op("dve", lambda e: e.reciprocal(rs2[:], sd2[:]), reads=["sd2"], writes=["rs2"])
                      S.op("dve", lambda e: e.scalar_tensor_tensor(gn[:], gm[:], rs2[:], gmG[:], ALU.mult, ALU.mult),
                           reads=["gm", "rs2", "tabA"], writes=[("gmn", j % 2)])

                  def gm_c(c_own, j):
                      gn = gmn2[j % 2]
                      transposes(lambda i: gn[:, i * 128:(i + 1) * 128], 8, [6], [("gmn", j % 2)])
                      g2 = j % 2
                      S.op("act", lambda e: e.activation(
                          out=gmTs[g2][:], in_=psbf[6].rearrange("p (k c) -> p k c", c=128), func=AF.Copy),
                          reads=[("ps", 6)], writes=[("gmTs", g2)])
                      S.dma(STQ, gmT_scr[:, :, c_own * 128:(c_own + 1) * 128], gmTs[g2][:],
                            reads=[("gmTs", g2)], writes=[("gmT_scr", c_own)])

                  hoisted = {}

                  def ln_pair(tt, jp):
                      halo_ = tt < THALO // TT
                      items = []
                      sls = []
                      for j in (2 * jp, 2 * jp + 1):
                          c = tt * 4 + j
                          sl = xslot["n"] % 4
                          xslot["n"] += 1
                          sls.append((j, sl))
                          S.dma("sp", xt[sl][:], x[c * 128:(c + 1) * 128, :], writes=[("xt", sl)])
                          if halo_:
                              items.append(dict(xin=xt[sl][:], xkey=("xt", sl), gt=lnG[:], bt=lnB[:], tkeys=["tabA"],
                                                out_bf=hbf[sl][:], okey_b=("hbf", sl)))
                          else:
                              items.append(dict(xin=xt[sl][:], xkey=("xt", sl), gt=lnG[:], bt=lnB[:], tkeys=["tabA"],
                                                out_f32=xt[sl][:], okey_f=("xt", sl),
                                                out_bf=hbf[sl][:], okey_b=("hbf", sl)))
                      layer_norm_multi(lnstk, items)
                      if not halo_:
                          for (j, sl) in sls:
                              co = (tt - THALO // TT) * 4 + j
                              S.dma(STQ, h_scr[co * 128:(co + 1) * 128, :], xt[sl][:],
                                    reads=[("xt", sl)], writes=[("h_scr", co)])
                      return sls

                  def tr_pair(sls):
                      for (j, sl) in sls:
                          transposes(lambda i, sl=sl: hbf[sl][:, i * 128:(i + 1) * 128], 16, [6, 7], [("hbf", sl)])
                          S.op("act", lambda e, j=j: e.activation(
                              out=hT[:, 0:8, j * 128:(j + 1) * 128],
                              in_=psbf[6].rearrange("p (k c) -> p k c", c=128), func=AF.Copy),
                              reads=[("ps", 6)], writes=[("hT", j)])
                          S.op("dve", lambda e, j=j: e.tensor_copy(
                              hT[:, 8:16, j * 128:(j + 1) * 128],
                              psbf[7].rearrange("p (k c) -> p k c", c=128)),
                              reads=[("ps", 7)], writes=[("hT", j)])

                  qslot = {"n": 0}
                  vslot = {"n": 0}
                  xslot = {"n": 0}
                  for t in DBG.get('tilesA', range(TTOT // TT)):
                      halo = t < THALO // TT
                      tok0 = t * TT
                      sls0 = hoisted.pop((t, 0)) if (t, 0) in hoisted else ln_pair(t, 0)
                      tr_pair(sls0)
                      sls1 = hoisted.pop((t, 1)) if (t, 1) in hoisted else ln_pair(t, 1)
                      tr_pair(sls1)
                      hT_all = [("hT", j) for j in range(4)]
                      S.dma("sp", posi[:], pos[:, tok0:tok0 + TT], writes=["posi"])
                      S.op("dve", lambda e: e.tensor_copy(posf[:], posi[:]), reads=["posi"], writes=["posf"])
                      S.op("dve", lambda e: e.tensor_scalar(kf[:], posf[:], invf_c[:], INV_2PI, ALU.mult, ALU.mult),
                           reads=["posf", "invf_c"], writes=["kf"])
                      S.op("dve", lambda e: e.tensor_copy(ki[:], kf[:]), reads=["kf", "posf"], writes=["posi"])
                      S.op("dve", lambda e: e.tensor_copy(kf[:], ki[:]), reads=["posi"], writes=["kf"])
                      S.op("dve", lambda e: e.tensor_scalar(ang[:], posf[:], invf_c[:], None, ALU.mult),
                           reads=["posf", "invf_c"], writes=["ang"])
                      S.op("dve", lambda e: e.scalar_tensor_tensor(ang[:], kf[:], -CW1, ang[:], ALU.mult, ALU.add),
                           reads=["kf", "ang"], writes=["ang"])
                      S.op("dve", lambda e: e.scalar_tensor_tensor(ang[:], kf[:], -CW2, ang[:], ALU.mult, ALU.add),
                           reads=["kf", "ang"], writes=["ang"])
                      S.op("dve", lambda e: e.tensor_scalar(ang[:], ang[:], PI, -PI, ALU.min, ALU.max),
                           reads=["ang"], writes=["ang"])
                      S.op("act", lambda e: e.activation(out=rabs[:], in_=ang[:], func=AF.Abs),
                           reads=["ang"], writes=["kf"])
                      S.op("act", lambda e: e.activation(out=sinS[:], in_=ang[:], func=AF.Sin, scale=sgn_c[:]),
                           reads=["ang", "consts"], writes=["sinS"])
                      S.op("act", lambda e: e.activation(out=cosT[:], in_=rabs[:], func=AF.Sin, scale=-1.0, bias=hpi_c[:]),
                           reads=["kf", "consts"], writes=["cosT"])
                      ulist = [2, 3, 4, 5] if halo else list(range(10))
                      ulist = [u for u in ulist if u in DBG.get('units', ulist)]
                      for ui_, u in enumerate(ulist):
                          slot = wload(UNITS["w_in"][u][0])
                          if u < 4:
                              for hh in range(4):
                                  head = (u % 2) * 4 + hh
                                  b = acc_bank()
                                  mm_group(ps[b][:], [(wr[slot][:, kc, hh * 128:(hh + 1) * 128], hT[:, kc, :])
                                                      for kc in range(16)], [("wr", slot)] + hT_all, b)
                                  r = qslot["n"] % 2
                                  q3 = qslot["n"] % 3
                                  qslot["n"] += 1
                                  def e_rope(e, r=r, b=b):
                                      e.tensor_tensor(t1[r][0:64, :], ps[b][64:128, :], sinS[0:64, :], ALU.mult)
                                      e.tensor_tensor(t1[r][64:128, :], ps[b][0:64, :], sinS[64:128, :], ALU.mult)
                                      return e.tensor_tensor(xs[r][:], ps[b][:], cosT[:], ALU.mult)
                                  S.op("dve", e_rope, reads=[("ps", b), "sinS", "cosT"], writes=[("t1", r), ("xs", r)])
                                  S.op("pool", lambda e, r=r, q3=q3: e.tensor_tensor(qst[q3][:], t1[r][:], xs[r][:], ALU.add),
                                       reads=[("t1", r), ("xs", r)], writes=[("qst", q3)])
                                  if u < 2:
                                      dst = qT_scr[head, :, tok0 - THALO:tok0 - THALO + TT]
                                      dk = ("qT_scr", head, t)
                                  else:
                                      dst = kT_scr[head, :, tok0:tok0 + TT]
                                      dk = ("kT_scr", head, t)
                                  if DBG.get('qk_dma', True):
                                      S.dma(STQ, dst, qst[q3][:], reads=[("qst", q3)], writes=[dk])
                          else:
                              for j in range(4):
                                  c = t * 4 + j
                                  b = acc_bank()
                                  mm_group(ps[b][:], [(hT[:, kc, j * 128:(j + 1) * 128], wr[slot][:, kc, :])
                                                      for kc in range(16)], [("wr", slot), ("hT", j)], b)
                                  if u < 6:
                                      v3 = vslot["n"] % 3
                                      vslot["n"] += 1
                                      vc = hv_c if halo else one_c
                                      S.op("act", lambda e, b=b, v3=v3, vc=vc: e.activation(
                                          out=vst[v3][:, :, 0:128], in_=ps[b][:].rearrange("p (h e) -> p h e", e=128),
                                          func=AF.Copy, scale=vc[:]),
                                          reads=[("ps", b), "hv_c", "consts"], writes=[("vst", v3)])
                                      S.op("pool", lambda e, v3=v3, vc=vc: e.tensor_copy(
                                          vst[v3][:, :, 128:129], vc[:].unsqueeze(1).to_broadcast([128, 4, 1])),
                                          reads=["hv_c", "consts"], writes=[("vst", v3)])
                                      S.dma(STQ, v_scr[c * 128:(c + 1) * 128, (u - 4) * 516:(u - 3) * 516],
                                            vst[v3][:].rearrange("p h e -> p (h e)"),
                                            reads=[("vst", v3)], writes=[("v_scr", c, u)])
                                  elif u < 8:
                                      S.op("act", lambda e, b=b, j=j, u=u: e.activation(
                                          out=ugl[:, j, (u - 6) * 512:(u - 5) * 512], in_=ps[b][:], func=AF.Gelu),
                                          reads=[("ps", b)], writes=[("ugl", j)])
                                  else:
                                      S.op("act", lambda e, b=b, j=j, u=u: e.activation(
                                          out=ggl[:, j, (u - 8) * 512:(u - 7) * 512], in_=ps[b][:], func=AF.Gelu),
                                          reads=[("ps", b)], writes=[("ggl", j)])
                          if ui_ in (1, 2) and 'tilesA' not in DBG and t + 1 < TTOT // TT:
                              hoisted[(t + 1, ui_ - 1)] = ln_pair(t + 1, ui_ - 1)
                          if pend["t"] is not None and not halo:
                              tp_ = pend["t"]
                              if 0 <= ui_ - 2 < 4:
                                  gm_c((tp_ - THALO // TT) * 4 + ui_ - 2, ui_ - 2)
                              if 0 <= ui_ - 1 < 4:
                                  gm_b(ui_ - 1)
                              if 0 <= ui_ < 4:
                                  gm_a(ui_)
                      if halo or not DBG.get('gmlp', True):
                          continue
                      pend["t"] = t
                  if pend["t"] is not None:
                      for j in range(4):
                          gm_a(j)
                          gm_b(j)
                          gm_c((pend["t"] - THALO // TT) * 4 + j, j)
                  S.barrier()
            if stop_after == 'A':
                raise _Stop

            with contextlib.ExitStack() as pb:
              if 'B' not in DBG.get('skip', []):
                  KT = sb(pb, "KT", [128, NH, 4096], BF16)
                  QT = sb(pb, "QT", [128, NH, 2048], BF16)
                  Vb = [sb(pb, "Vb%d" % i, [128, 2, VW], BF16) for i in range(4)]
                  PT = [sb(pb, "PT%d" % i, [128, 4, 256], BF16) for i in range(2)]
                  stg = [sb(pb, "stg%d" % i, [128, NH, 129]) for i in range(2)]
                  blk = {"n": 0, "g": 0}
                  scale_qk = float(128.0 ** -0.5)
                  cstB = [sb(pb, "cstB%d" % i, [128, 8, 512]) for i in range(3)]
                  cbfB = [sb(pb, "cbfB%d" % i, [128, 8, 512], BF16) for i in range(3)]
                  cstep, cdrain, cload = make_conv(cstB, cbfB, "B", ["dve"])
                  cload()
                  cload()
                  for sp_ in range(2):
                      for h in range(NH):
                          S.dma("sp", KT[:, h, :], kT_scr[h, :, sp_ * 2048:sp_ * 2048 + 4096],
                                writes=[("KT", h)])
                          S.dma("sp", QT[:, h, :], qT_scr[h, :, sp_ * 2048:(sp_ + 1) * 2048],
                                writes=[("QT", h)])
                      for dix, d in enumerate(BRANCH_D):
                          for r in range(d):
                              for qb in range(16 // d):
                                  n = blk["n"]
                                  blk["n"] += 1
                                  vs = n % 4
                                  ss = n % 2
                                  q0 = r + qb * 128 * d
                                  span_ = 127 * d + 1
                                  qsl = slice(q0, q0 + span_, d)
                                  kc0 = 2048 + q0
                                  ksl = (slice(kc0 - 128 * d, kc0 - 128 * d + span_, d), slice(kc0, kc0 + span_, d))
                                  tk = sp_ * 2048 + kc0
                                  S.dma("sp", Vb[vs][:, 0, :], v_scr[tk - 128 * d:tk - 128 * d + span_:d, :], writes=[("Vb", vs)])
                                  S.dma("sp", Vb[vs][:, 1, :], v_scr[tk:tk + span_:d, :], writes=[("Vb", vs)])
                                  cstep()
                                  for hg in range(2):
                                      g = blk["g"]
                                      blk["g"] += 1
                                      sbk = (0, 1) if g % 2 == 0 else (2, 3)
                                      obk = (4, 5) if g % 2 == 0 else (6, 7)
                                      pt = PT[g % 2]
                                      def e_qk(e, hg=hg, sbk=sbk, ksl=ksl, qsl=qsl):
                                          last = None
                                          for hh in range(4):
                                              h = hg * 4 + hh
                                              for ti in range(2):
                                                  o = (hh % 2) * 256 + ti * 128
                                                  last = e.matmul(ps[sbk[hh // 2]][:, o:o + 128], KT[:, h, ksl[ti]],
                                                                  QT[:, h, qsl], start=True, stop=True)
                                          return last
                                      S.op("pe", e_qk, reads=[("KT", h) for h in range(hg * 4, hg * 4 + 4)] +
                                           [("QT", h) for h in range(hg * 4, hg * 4 + 4)],
                                           writes=[("ps", sbk[0]), ("ps", sbk[1])])
                                      for bi in range(2):
                                          S.op("act", lambda e, bi=bi, pt=pt, sbk=sbk: e.activation(
                                              out=pt[:, bi * 2:bi * 2 + 2, :].rearrange("p h k -> p (h k)"),
                                              in_=ps[sbk[bi]][:], func=AF.Exp, scale=scale_qk),
                                              reads=[("ps", sbk[bi])], writes=[("PT", g % 2, bi)])
                                      S.op("pool", lambda e, pt=pt: e.tensor_tensor(
                                          pt[:], pt[:], mask2[:].unsqueeze(1).to_broadcast([128, 4, 256]), ALU.mult),
                                          reads=[("PT", g % 2, 0), ("PT", g % 2, 1), "mask2"],
                                          writes=[("PT", g % 2, 0), ("PT", g % 2, 1)])
                                      o0 = sp_ * 2048 + q0
                                      def tail(hg=hg, obk=obk, pt=pt, vs=vs, ss=ss, g=g, dix=dix, o0=o0, span_=span_, d=d,
                                               sp_=sp_, r=r, qb=qb):
                                          def e_pv(e):
                                              last = None
                                              for hh in range(4):
                                                  h = hg * 4 + hh
                                                  o = (hh % 2) * 256
                                                  for ti in range(2):
                                                      last = e.matmul(ps[obk[hh // 2]][:, o:o + 129],
                                                                      pt[:, hh, ti * 128:(ti + 1) * 128],
                                                                      Vb[vs][:, ti, h * 129:(h + 1) * 129],
                                                                      start=(ti == 0), stop=(ti == 1))
                                              return last
                                          S.op("pe", e_pv, reads=[("PT", g % 2, 0), ("PT", g % 2, 1), ("Vb", vs)],
                                               writes=[("ps", obk[0]), ("ps", obk[1])])
                                          for bi in range(2):
                                              S.op("dve", lambda e, bi=bi: e.tensor_copy(
                                                  stg[ss][:, hg * 4 + bi * 2:hg * 4 + bi * 2 + 2, :],
                                                  ps[obk[bi]][:].rearrange("p (h k) -> p h k", k=256)[:, :, 0:129]),
                                                  reads=[("ps", obk[bi])], writes=[("stg", ss)])
                                          if hg == 1:
                                              S.dma(STQ, attn_scr[dix, o0:o0 + span_:d, :],
                                                    stg[ss][:].rearrange("p h k -> p (h k)"),
                                                    reads=[("stg", ss)], writes=[("attn_scr", dix, sp_, r, qb)])
                                      if blk.get("tail") is not None:
                                          blk["tail"]()
                                      blk["tail"] = tail
                  blk["tail"]()
                  cdrain()
                  S.barrier()
            if stop_after == 'B':
                raise _Stop

            with contextlib.ExitStack() as pm:
              if 'B2' not in DBG.get('skip', []):
                  aG = sb(pm, "aG", [128, 1024])
                  av = [[sb(pm, "av%d_%d" % (i, k), [128, NH, 129]) for k in range(3)] for i in range(2)]
                  rden = sb(pm, "rden", [128, NH])
                  at = sb(pm, "at", [128, 1024])
                  ajunk = sb(pm, "ajunk", [128, 1024], BF16)
                  atn = sb(pm, "atn", [128, 1024], BF16)
                  aTs = [sb(pm, "aTs%d" % i, [128, 8, 128], BF16) for i in range(2)]
                  ssq = sb(pm, "ssqm", [128, 1])
                  sd2 = sb(pm, "sd2m", [128, 1])
                  rs2 = sb(pm, "rs2m", [128, 1])
                  S.dma("sp", aG[:], tabs["attn_out_g"], writes=["aG"])
                  cstM = [sb(pm, "cstM%d" % i, [128, 8, 512]) for i in range(3)]
                  cbfM = [sb(pm, "cbfM%d" % i, [128, 8, 512], BF16) for i in range(3)]
                  cstep, cdrain, cload = make_conv(cstM, cbfM, "M", ["act", "pool"])
                  cload()
                  cload()
                  for c in range(TOWN // 128):
                      s2 = c % 2
                      cstep()
                      cstep()
                      for k in range(3):
                          S.dma("sp", av[s2][k][:].rearrange("p h k -> p (h k)"), attn_scr[k, c * 128:(c + 1) * 128, :],
                                writes=[("av", s2, k)])
                      S.op("dve", lambda e, s2=s2: e.tensor_tensor(av[s2][0][:], av[s2][0][:], av[s2][1][:], ALU.add),
                           reads=[("av", s2, 0), ("av", s2, 1)], writes=[("av", s2, 0)])
                      S.op("dve", lambda e, s2=s2: e.tensor_tensor(av[s2][0][:], av[s2][0][:], av[s2][2][:], ALU.add),
                           reads=[("av", s2, 0), ("av", s2, 2)], writes=[("av", s2, 0)])
                      S.op("dve", lambda e, s2=s2: e.reciprocal(rden[:].unsqueeze(2), av[s2][0][:, :, 128:129]),
                           reads=[("av", s2, 0)], writes=["rden"])
                      S.op("dve", lambda e, s2=s2: e.tensor_tensor(
                          at[:].rearrange("p (h e) -> p h e", e=128), av[s2][0][:, :, 0:128],
                          rden[:].unsqueeze(2).to_broadcast([128, NH, 128]), ALU.mult),
                          reads=[("av", s2, 0), "rden"], writes=["at"])
                      S.op("pool", lambda e: e.memset(ssq[:], 0.0), writes=["ssqm"])
                      S.op("act", lambda e: e.activation(out=ajunk[:], in_=at[:], func=AF.Square, accum_out=ssq[:]),
                           reads=["at", "ssqm"], writes=["ssqm", "ajunk"])
                      S.op("act", lambda e: e.activation(out=sd2[:], in_=ssq[:], func=AF.Sqrt, bias=eps_c[:],
                                                         scale=1.0 / 1024), reads=["ssqm"], writes=["sd2m"])
                      S.op("dve", lambda e: e.reciprocal(rs2[:], sd2[:]), reads=["sd2m"], writes=["rs2m"])
                      S.op("dve", lambda e: e.scalar_tensor_tensor(atn[:], at[:], rs2[:], aG[:], ALU.mult, ALU.mult),
                           reads=["at", "rs2m", "aG"], writes=["atn"])
                      transposes(lambda i: atn[:, i * 128:(i + 1) * 128], 8, [6 + s2], ["atn"])
                      S.op("act", lambda e, s2=s2: e.activation(
                          out=aTs[s2][:], in_=psbf[6 + s2].rearrange("p (k c) -> p k c", c=128), func=AF.Copy),
                          reads=[("ps", 6 + s2)], writes=[("aTs", s2)])
                      S.dma(STQ, attnT_scr[:, :, c * 128:(c + 1) * 128], aTs[s2][:],
                            reads=[("aTs", s2)], writes=[("attnT_scr", c)])
                  while cpos["next"] < len(cjobs):
                      cstep()
                  cdrain()
                  S.barrier()
            if stop_after == 'B2':
                raise _Stop

            with contextlib.ExitStack() as pc:
                res = sb(pc, "res", [128, 4, D])
                hbf = [sb(pc, "hbfC%d" % i, [128, D], BF16) for i in range(4)]
                TB = [sb(pc, "TB%d" % i, [128, 16, TT], BF16) for i in range(2)]
                actT = sb(pc, "actT", [128, 24, TT], BF16)
                tabG = sb(pc, "tabG", [128, D])
                tabB = sb(pc, "tabB", [128, D])
                PTx = [sb(pc, "PTx%d" % i, [128, 2, TT], BF16) for i in range(2)]
                rdx = [sb(pc, "rdx%d" % i, [128, TT]) for i in range(2)]
                sg = [sb(pc, "sg%d" % i, [128, TT]) for i in range(2)]
                lnstk = (sb(pc, "bnstC", [128, 4, 4, 6]), sb(pc, "mvC", [128, 4, 2]), sb(pc, "sdC", [128, 4]),
                         sb(pc, "rstdC", [128, 4]), sb(pc, "nmrC", [128, 4]))
                accs["nb"] = 6
                wr = [sb(pc, "wrC%d" % i, [128, 16, 512], BF16) for i in range(3)]
                wstate["wr"] = wr
                wstate["n"] = 0
                KmemT = sb(pc, "KmemT", [128, 16, MEM], BF16)
                Vmem = sb(pc, "Vmem", [128, 2, D], BF16)
                S.dma("sp", KmemT[:].rearrange("p k m -> p (k m)"), KmemT_scr, writes=["KmemT"])
                S.dma("sp", Vmem[:].rearrange("p t d -> p (t d)"), Vmem_scr, writes=["Vmem"])
                scale_x = float(512.0 ** -0.5)
                cnt = {"pt": 0, "sg": 0, "hb": 0}

                def load_tabs(gname, bname):
                    S.dma("sp", tabG[:], tabs[gname], writes=["tabC"])
                    S.dma("sp", tabB[:], tabs[bname], writes=["tabC"])

                def ln_and_transpose(dstT, dkey, want_f32_keep=True):
                    items = [dict(xin=res[:, j, :], xkey=("res", j), gt=tabG[:], bt=tabB[:], tkeys=["tabC"],
                                  out_f32=res[:, j, :], okey_f=("res", j), out_bf=hbf[j][:], okey_b=("hbfC", j))
                             for j in range(4)]
                    layer_norm_multi(lnstk, items)
                    for j in range(4):
                        bk = [6, 7] if j % 2 == 0 else [4, 5]
                        transposes(lambda i, j=j: hbf[j][:, i * 128:(i + 1) * 128], 16, bk, [("hbfC", j)])
                        S.op("act", lambda e, j=j, bk=bk: e.activation(
                            out=dstT[:, 0:8, j * 128:(j + 1) * 128],
                            in_=psbf[bk[0]].rearrange("p (k c) -> p k c", c=128), func=AF.Copy),
                            reads=[("ps", bk[0])], writes=[(dkey, j)])
                        S.op("dve", lambda e, j=j, bk=bk: e.tensor_copy(
                            dstT[:, 8:16, j * 128:(j + 1) * 128],
                            psbf[bk[1]].rearrange("p (k c) -> p k c", c=128)),
                            reads=[("ps", bk[1])], writes=[(dkey, j)])

                def ln_plain():
                    items = [dict(xin=res[:, j, :], xkey=("res", j), gt=tabG[:], bt=tabB[:], tkeys=["tabC"],
                                  out_f32=res[:, j, :], okey_f=("res", j)) for j in range(4)]
                    layer_norm_multi(lnstk, items)

                def proj_tokmajor(srcT, skey, wname, first=True):
                    def grp(cg, slot, j):
                        b = acc_bank()
                        mm_group(ps[b][:], [(srcT[:, kc, j * 128:(j + 1) * 128], wr[slot][:, kc, :])
                                            for kc in range(16)], [("wr", slot), (skey, j)], b)
                        S.op("dve", lambda e: e.scalar_tensor_tensor(
                            res[:, j, cg * 512:(cg + 1) * 512], res[:, j, cg * 512:(cg + 1) * 512], ALPHA,
                            ps[b][:], ALU.mult, ALU.add),
                            reads=[("ps", b), ("res", j)], writes=[("res", j)])
                    for cg in range(2):
                        slot = wload(UNITS[wname][cg][0])
                        for j in range(4):
                            grp(cg, slot, j)
                    slot2 = wload(UNITS[wname][2][0])
                    slot3 = wload(UNITS[wname][3][0])
                    for j in range(4):
                        grp(2, slot2, j)
                        grp(3, slot3, j)

                mixT_done = set()

                def load_mixT(tt):
                    oo = tt * TT
                    S.dma("sp", TB[0][:, 0:8, :], attnT_scr[:, :, oo:oo + TT], writes=[("TA", j) for j in range(4)])
                    S.dma("sp", TB[0][:, 8:16, :], gmT_scr[:, :, oo:oo + TT], writes=[("TA", j) for j in range(4)])
                    mixT_done.add(tt)

                for t in DBG.get('tilesC', range(TOWN // TT)):
                    o0 = t * TT
                    A_, B_ = TB[0], TB[1]
                    if t not in mixT_done:
                        load_mixT(t)
                    for j in range(4):
                        c = o0 // 128 + j
                        S.dma("sp", res[:, j, :], h_scr[c * 128:(c + 1) * 128, :], writes=[("res", j)])
                    proj_tokmajor(A_, "TA", "w_mix")
                    if DBG.get('stepC', 99) <= 1:
                        continue
                    load_tabs("ln1_g", "ln1_b")
                    ln_and_transpose(B_, "TB")
                    if DBG.get('stepC', 99) <= 2:
                        continue
                    for cg in range(4):
                        slot = wload(UNITS["w_xq"][cg][0])
                        for ec in range(4):
                            b = acc_bank()
                            mm_group(ps[b][:], [(wr[slot][:, kc, ec * 128:(ec + 1) * 128], B_[:, kc, :])
                                                for kc in range(16)], [("wr", slot)] + [("TB", j) for j in range(4)], b)
                            S.op("act", lambda e, b=b, cg=cg, ec=ec: e.activation(
                                out=A_[:, cg * 4 + ec, :], in_=ps[b][:], func=AF.Copy),
                                reads=[("ps", b)], writes=[("TAq", cg * 4 + ec)] + [("TA", j) for j in range(4)])
                    TAall = [("TA", j) for j in range(4)]
                    if DBG.get('stepC', 99) <= 3:
                        continue
                    for hx in range(4):
                        pslot = cnt["pt"] % 2
                        cnt["pt"] += 1
                        for mt in range(2):
                            b = acc_bank()
                            mm_group(ps[b][:], [(KmemT[:, hx * 4 + ec, mt * 128:(mt + 1) * 128], A_[:, hx * 4 + ec, :])
                                                for ec in range(4)], ["KmemT"] + TAall, b)
                            S.op("act", lambda e, b=b, mt=mt, pslot=pslot: e.activation(
                                out=PTx[pslot][:, mt, :], in_=ps[b][:], func=AF.Exp, scale=scale_x),
                                reads=[("ps", b)], writes=[("PTx", pslot)])
                        if DBG.get('attnC', 9) <= 1:
                            continue
                        b = acc_bank()
                        mm_group(ps[b][:], [(onesbf[:], PTx[pslot][:, mt, :]) for mt in range(2)],
                                 [("PTx", pslot), "onesbf"], b)
                        S.op("dve", lambda e, b=b, pslot=pslot: e.reciprocal(rdx[pslot][:], ps[b][:]),
                             reads=[("ps", b)], writes=[("rdx", pslot)])
                        if DBG.get('attnC', 9) <= 2:
                            continue
                        for ec in range(4):
                            b = acc_bank()
                            col = hx * 512 + ec * 128
                            mm_group(ps[b][:], [(Vmem[:, mt, col:col + 128], PTx[pslot][:, mt, :]) for mt in range(2)],
                                     [("PTx", pslot), "Vmem"], b)
                            if DBG.get('pvcopy', False):
                                S.op("dve", lambda e, b=b, pslot=pslot, hx=hx, ec=ec: e.tensor_copy(
                                    B_[:, hx * 4 + ec, :], ps[b][:]),
                                    reads=[("ps", b), ("rdx", pslot)], writes=[("TB", j) for j in range(4)])
                                continue
                            if DBG.get('pvswap', True):
                                S.op("dve", lambda e, b=b, pslot=pslot, hx=hx, ec=ec: e.tensor_tensor(
                                    B_[:, hx * 4 + ec, :], rdx[pslot][:], ps[b][:], ALU.mult),
                                    reads=[("ps", b), ("rdx", pslot)], writes=[("TB", j) for j in range(4)])
                                continue
                            S.op("dve", lambda e, b=b, pslot=pslot, hx=hx, ec=ec: e.tensor_tensor(
                                B_[:, hx * 4 + ec, :], ps[b][:], rdx[pslot][:], ALU.mult),
                                reads=[("ps", b), ("rdx", pslot)], writes=[("TB", j) for j in range(4)])
                    if DBG.get('stepC', 99) <= 4:
                        continue
                    proj_tokmajor(B_, "TB", "w_xo")
                    if DBG.get('stepC', 99) <= 5:
                        continue
                    load_tabs("ln2_g", "ln2_b")
                    ln_and_transpose(A_, "TA")
                    if DBG.get('stepC', 99) <= 6:
                        continue
                    for half, (u0, u1) in enumerate(((0, 6), (6, 11))):
                        for u in range(u0, u1):
                            sg_slot = wload(UNITS["w_gate"][u][0])
                            su_slot = wload(UNITS["w_up"][u][0])
                            for fi in range(4):
                                fl = (u - u0) * 4 + fi
                                bg = acc_bank()
                                mm_group(ps[bg][:], [(wr[sg_slot][:, kc, fi * 128:(fi + 1) * 128], A_[:, kc, :])
                                                     for kc in range(16)], [("wr", sg_slot)] + TAall, bg)
                                bu = acc_bank()
                                mm_group(ps[bu][:], [(wr[su_slot][:, kc, fi * 128:(fi + 1) * 128], A_[:, kc, :])
                                                     for kc in range(16)], [("wr", su_slot)] + TAall, bu)
                                s_ = cnt["sg"] % 2
                                cnt["sg"] += 1
                                S.op("act", lambda e, bg=bg, s_=s_: e.activation(out=sg[s_][:], in_=ps[bg][:], func=AF.Silu),
                                     reads=[("ps", bg)], writes=[("sg", s_)])
                                S.op("dve", lambda e, bu=bu, s_=s_, fl=fl: e.tensor_tensor(
                                    actT[:, fl, :], sg[s_][:], ps[bu][:], ALU.mult),
                                    reads=[("ps", bu), ("sg", s_)], writes=[("actT", fl)])
                        nfl = (u1 - u0) * 4
                        for cgo in range(4):
                            banks = [acc_bank() for _ in range(4)]
                            kus = UNITS["w_down"][cgo][half * 2:half * 2 + 2]
                            fl0 = 0
                            for ui, unit in enumerate(kus):
                                slot = wload(unit)
                                nkc = unit[2]
                                for j in range(4):
                                    def e_dn(e, j=j, slot=slot, nkc=nkc, fl0=fl0, ui=ui, b=banks[j]):
                                        last = None
                                        for kc in range(nkc):
                                            last = e.matmul(ps[b][:], actT[:, fl0 + kc, j * 128:(j + 1) * 128],
                                                            wr[slot][:, kc, :], start=(ui == 0 and kc == 0),
                                                            stop=(ui == 1 and kc == nkc - 1))
                                        return last
                                    S.op("pe", e_dn, reads=[("wr", slot)] + [("actT", fl0 + kc) for kc in range(nkc)],
                                         writes=[("ps", banks[j])])
                                fl0 += nkc
                            assert fl0 == nfl
                            for j in range(4):
                                b = banks[j]
                                if half == 0:
                                    S.op("dve", lambda e, b=b, j=j, cgo=cgo: e.scalar_tensor_tensor(
                                        res[:, j, cgo * 512:(cgo + 1) * 512], res[:, j, cgo * 512:(cgo + 1) * 512],
                                        ALPHA, ps[b][:], ALU.mult, ALU.add),
                                        reads=[("ps", b), ("res", j)], writes=[("res", j)])
                                else:
                                    S.op("dve", lambda e, b=b, j=j, cgo=cgo: e.tensor_tensor(
                                        res[:, j, cgo * 512:(cgo + 1) * 512], res[:, j, cgo * 512:(cgo + 1) * 512],
                                        ps[b][:], ALU.add),
                                        reads=[("ps", b), ("res", j)], writes=[("res", j)])
                    if DBG.get('stepC', 99) <= 7:
                        continue
                    if t + 1 < TOWN // TT and 'tilesC' not in DBG:
                        load_mixT(t + 1)
                        for cgp in range(3):
                            wprefetch(UNITS["w_mix"][cgp][0])
                    load_tabs("ln3_g", "ln3_b")
                    ln_plain()
                    for j in range(4):
                        S.dma(STQ, out[o0 + j * 128:o0 + (j + 1) * 128, :], res[:, j, :],
                              reads=[("res", j)], writes=[("out", t, j)])
                S.barrier()
            if stop_after == 'C':
                raise _Stop
        except _Stop:
            S.barrier()
        nc._n_sched_inst = S.ninst
    return nc


_CACHE = {}


def _in_maps(inputs):
    x = np.asarray(inputs["x"], dtype=np.float32)
    mem = np.asarray(inputs["mem"], dtype=np.float32)
    positions = np.asarray(inputs["positions"], dtype=np.int32)
    half = 64
    invf = (np.float32(10000.0) ** (-np.arange(half, dtype=np.float32) / np.float32(half))).astype(np.float32)
    invf_col = np.ascontiguousarray(np.concatenate([invf, invf])[:, None])

    def rep(v):
        return np.ascontiguousarray(np.broadcast_to(np.asarray(v, np.float32).reshape(1, -1), (128, v.size)))

    shared = {"invf": invf_col}
    for nm in ("ln_in_g", "ln_in_b", "ln1_g", "ln1_b", "ln2_g", "ln2_b", "ln3_g", "ln3_b",
               "sgu_norm_g", "sgu_norm_b", "attn_out_g", "gmlp_out_g"):
        shared[nm] = rep(np.asarray(inputs[nm], np.float32).reshape(-1))
    shared["w_spatial"] = np.ascontiguousarray(np.asarray(inputs["w_spatial"], np.float32)[0])
    shared["b_spatial"] = np.ascontiguousarray(np.asarray(inputs["b_spatial"], np.float32)[0])
    names = {"w_in": "w_in", "w_mix": "w_mix_out", "w_xq": "w_xq", "w_xo": "w_xo", "w_gate": "w_ffn_gate",
             "w_up": "w_ffn_up", "w_down": "w_ffn_down", "w_xk": "w_xk", "w_xv": "w_xv"}
    for k, src in names.items():
        shared[k] = np.ascontiguousarray(np.asarray(inputs[src], np.float32)[0])
    maps = []
    for c in range(8):
        b, s = divmod(c, 4)
        lo = s * TOWN - THALO
        xc = np.zeros((TTOT, D), np.float32)
        pc = np.zeros((TTOT,), np.int32)
        if s == 0:
            xc[THALO:] = x[b, 0:TOWN]
            pc[THALO:] = positions[b, 0:TOWN]
            hvv = 0.0
        else:
            xc[:] = x[b, lo:lo + TTOT]
            pc[:] = positions[b, lo:lo + TTOT]
            hvv = 1.0
        m = dict(shared)
        m["x"] = xc
        m["mem"] = np.ascontiguousarray(mem[b])
        m["pos"] = np.ascontiguousarray(np.broadcast_to(pc[None, :], (128, TTOT)))
        m["hv"] = np.full((128, 1), hvv, np.float32)
        maps.append(m)
    return maps


def kernel(**inputs):
    if "nc" not in _CACHE:
        _CACHE["nc"] = build_program(debug=False)
    nc = _CACHE["nc"]
    maps = _in_maps(inputs)
    res = run_bass_kernel_spmd(nc, maps, core_ids=list(range(8)))
    outp = np.empty((2, 4 * TOWN, D), np.float32)
    for c in range(8):
        b, s = divmod(c, 4)
        outp[b, s * TOWN:(s + 1) * TOWN] = np.asarray(res.results[c]["out"], np.float32)
    return outp
```

```python
import contextlib
import numpy as np
import concourse.bass as bass
import concourse.mybir as mybir
from concourse.bass_utils import run_bass_kernel_spmd

F32 = mybir.dt.float32
BF16 = mybir.dt.bfloat16
I32 = mybir.dt.int32
AF = mybir.ActivationFunctionType
ALU = mybir.AluOpType

D = 2048
NH = 8
TOWN = 4096
THALO = 2048
TTOT = TOWN + THALO
TT = 512
DFF = 5632
MEM = 256
ALPHA = float(2.0 ** 0.25)
EPS = 1e-5
VW = NH * 129
PI = float(np.float32(np.pi))
HALF_PI = float(np.float32(np.pi / 2))
INV_2PI = float(np.float32(1.0 / (2 * np.pi)))
CW1 = 6.28125
CW2 = float(np.float32(2 * np.pi - 6.28125))
BRANCH_D = (1, 4, 16)

WSPEC = [("w_in", 2048, 5120), ("w_mix", 2048, 2048), ("w_xq", 2048, 2048),
         ("w_xo", 2048, 2048), ("w_gate", 2048, DFF), ("w_up", 2048, DFF),
         ("w_down", DFF, 2048)]
DOWN_KSPLIT = [(0, 12), (12, 12), (24, 10), (34, 10)]


def _unit_table():
    units = {}
    uid = 0
    for name, K, N in WSPEC:
        ks = DOWN_KSPLIT if name == "w_down" else [(0, K // 128)]
        tab = []
        for cg in range(N // 512):
            row = []
            for (k0, nkc) in ks:
                row.append((uid, k0, nkc))
                uid += 1
            tab.append(row)
        units[name] = tab
    return units, uid


UNITS, NUNITS = _unit_table()
DBG = {}
STQ = "act"


class Sched:
    def __init__(self, nc, es):
        self.nc = nc
        self.eng = {"pe": nc.tensor, "act": nc.scalar, "dve": nc.vector, "pool": nc.gpsimd,
                    "sp": nc.sync}
        self.sem = {e: es.enter_context(nc.semaphore("sem_" + e)) for e in ("pe", "act", "dve", "pool")}
        self.cnt = {e: 0 for e in self.sem}
        self.seen = {e: {} for e in self.eng}
        self.lastw = {}
        self.readers = {}
        nd = {"sp": 28, "pool": 10, "act": 6}
        self.dsem = {q: [es.enter_context(nc.semaphore("d_%s_%d" % (q, i))) for i in range(n)]
                     for q, n in nd.items()}
        self.dcnt = {q: [0] * n for q, n in nd.items()}
        self.dnext = {q: 0 for q in nd}
        self.ninst = 0

    def _need(self, eng, tok):
        kind, ident, n = tok
        if kind == "c" and ident == eng:
            return
        key = (kind, ident)
        if self.seen[eng].get(key, 0) >= n:
            return
        self.seen[eng][key] = n
        sem = self.sem[ident] if kind == "c" else self.dsem[ident[0]][ident[1]]
        self.eng[eng].wait_ge(sem, n)
        self.ninst += 1

    def _need_same(self, eng, tok):
        kind, ident, n = tok
        key = (kind, ident)
        if self.seen[eng].get(key, 0) >= n:
            return
        self.seen[eng][key] = n
        self.eng[eng].wait_ge(self.sem[ident], n)
        self.ninst += 1

    def _deps(self, eng, reads, writes):
        for k in reads:
            w = self.lastw.get(k)
            if w is not None:
                if w[0] == "c" and w[1] == eng:
                    if eng != "pe":
                        self._need_same(eng, w)
                else:
                    self._need(eng, w)
        for k in writes:
            w = self.lastw.get(k)
            if w is not None:
                self._need(eng, w)
            for (kind, ident), n in self.readers.get(k, {}).items():
                self._need(eng, (kind, ident, n))

    def _record(self, tok, reads, writes):
        key = (tok[0], tok[1])
        for k in reads:
            self.readers.setdefault(k, {})[key] = tok[2]
        for k in writes:
            self.lastw[k] = tok
            self.readers[k] = {}

    def op(self, eng, emit, reads=(), writes=()):
        self._deps(eng, reads, writes)
        inst = emit(self.eng[eng])
        inst.then_inc(self.sem[eng], 1)
        self.cnt[eng] += 1
        self.ninst += 1
        tok = ("c", eng, self.cnt[eng])
        self._record(tok, reads, writes)
        return tok

    def dma(self, q, out, in_, reads=(), writes=()):
        self._deps(q, reads, writes)
        i = self.dnext[q]
        self.dnext[q] = (i + 1) % len(self.dsem[q])
        if self.dcnt[q][i] > 0:
            self._need(q, ("d", (q, i), self.dcnt[q][i]))
        self.eng[q].dma_start(out=out, in_=in_).then_inc(self.dsem[q][i], 16)
        self.dcnt[q][i] += 16
        self.ninst += 1
        tok = ("d", (q, i), self.dcnt[q][i])
        self._record(tok, reads, writes)
        return tok

    def barrier(self, engines=("pe", "act", "dve", "pool", "sp")):
        for e in engines:
            for o in self.sem:
                if self.cnt[o] > 0:
                    if o == e:
                        continue
                    self._need(e, ("c", o, self.cnt[o]))
            for q in self.dsem:
                for i, n in enumerate(self.dcnt[q]):
                    if n > 0:
                        self._need(e, ("d", (q, i), n))
        self.lastw = {}
        self.readers = {}


class _Stop(Exception):
    pass


def build_program(debug=False, stop_after=None):
    nc = bass.Bass("TRN2", target_bir_lowering=False)

    def din(name, shape, dt=F32):
        return nc.dram_tensor(name, list(shape), dt, kind="ExternalInput").ap()

    x = din("x", [TTOT, D])
    mem = din("mem", [MEM, D])
    pos = din("pos", [128, TTOT], I32)
    hv = din("hv", [128, 1])
    invf = din("invf", [128, 1])
    tabs = {}
    for nm in ("ln_in_g", "ln_in_b", "ln1_g", "ln1_b", "ln2_g", "ln2_b", "ln3_g", "ln3_b"):
        tabs[nm] = din(nm, [128, D])
    for nm in ("sgu_norm_g", "sgu_norm_b", "attn_out_g", "gmlp_out_g"):
        tabs[nm] = din(nm, [128, 1024])
    w_sp = din("w_spatial", [8, 128, 128])
    b_sp = din("b_spatial", [8, 128])
    wd = {}
    for name, K, N in WSPEC:
        wd[name] = din(name, [K, N])
    wd["w_xk"] = din("w_xk", [D, D])
    wd["w_xv"] = din("w_xv", [D, D])
    out = nc.dram_tensor("out", [TOWN, D], F32, kind="ExternalOutput").ap()

    skind = "ExternalOutput" if debug else "Internal"

    def dscr(name, shape, dt):
        return nc.dram_tensor(name, list(shape), dt, kind=skind).ap()

    wscr = dscr("wscr", [NUNITS, 128, 8192], BF16)
    qT_scr = dscr("qT_scr", [NH, 128, TOWN], BF16)
    kT_scr = dscr("kT_scr", [NH, 128, TTOT], BF16)
    v_scr = dscr("v_scr", [TTOT, VW], BF16)
    gmT_scr = dscr("gmT_scr", [128, 8, TOWN], BF16)
    attn_scr = dscr("attn_scr", [3, TOWN, VW], F32)
    attnT_scr = dscr("attnT_scr", [128, 8, TOWN], BF16)
    KmemT_scr = dscr("KmemT_scr", [128, 16 * MEM], BF16)
    h_scr = nc.dram_tensor("h_scr", [TOWN, D], F32).ap()
    Vmem_scr = dscr("Vmem_scr", [128, 2 * D], BF16)

    with contextlib.ExitStack() as es:
        S = Sched(nc, es)

        def sb(stack, name, shape, dt=F32):
            return stack.enter_context(nc.sbuf_tensor(name, list(shape), dt))

        ps = [es.enter_context(nc.psum_tensor("ps%d" % b, [128, 512], F32)) for b in range(8)]
        psbf = [p[:].bitcast(BF16) for p in ps]

        ident = sb(es, "ident", [128, 128], BF16)
        identf = sb(es, "identf", [128, 128])
        onesbf = sb(es, "onesbf", [128, 128], BF16)
        mask2 = sb(es, "mask2", [128, 256], BF16)
        maskf = sb(es, "maskf", [128, 256])
        eps_c = sb(es, "eps_c", [128, 1])
        hpi_c = sb(es, "hpi_c", [128, 1])
        sgn_c = sb(es, "sgn_c", [128, 1])
        one_c = sb(es, "one_c", [128, 1])
        hv_c = sb(es, "hv_c", [128, 1])
        invf_c = sb(es, "invf_c", [128, 1])
        wstate = {"n": 0, "wr": None}
        accs = {"n": 0, "nb": 4}

        def acc_bank():
            b = accs["n"] % accs["nb"]
            accs["n"] += 1
            return b

        def wload(unit):
            uid, k0, nkc = unit
            if uid in wstate.setdefault("pref", {}):
                return wstate["pref"].pop(uid)
            return wfetch(unit)

        def wprefetch(unit):
            wstate.setdefault("pref", {})[unit[0]] = wfetch(unit)

        def wfetch(unit):
            uid, k0, nkc = unit
            wr = wstate["wr"]
            slot = wstate["n"] % len(wr)
            wstate["n"] += 1
            S.dma("sp", wr[slot][:, 0:nkc, :],
                  wscr[uid, :, 0:nkc * 512].rearrange("p (k c) -> p k c", c=512),
                  reads=[("wscr", uid)], writes=[("wr", slot)])
            return slot

        def c_consts(e):
            e.memset(eps_c[:], EPS)
            e.memset(hpi_c[:], HALF_PI)
            e.memset(one_c[:], 1.0)
            e.memset(sgn_c[0:64, :], -1.0)
            e.memset(sgn_c[64:128, :], 1.0)
            e.memset(identf[:], 0.0)
            return e.memset(maskf[:], 1.0)
        S.op("pool", c_consts, writes=["consts", "identf", "maskf"])
        S.op("pool", lambda e: e.affine_select(out=identf[:], in_=identf[:], pattern=[[-1, 128]],
                                               compare_op=ALU.not_equal, fill=1.0, base=0,
                                               channel_multiplier=1),
             reads=["identf"], writes=["identf"])

        def c_masks(e):
            e.affine_select(out=maskf[:, 0:128], in_=maskf[:, 0:128], pattern=[[-1, 128]],
                            compare_op=ALU.is_ge, fill=0.0, base=0, channel_multiplier=1)
            return e.affine_select(out=maskf[:, 128:256], in_=maskf[:, 128:256], pattern=[[1, 128]],
                                   compare_op=ALU.is_ge, fill=0.0, base=0, channel_multiplier=-1)
        S.op("pool", c_masks, reads=["maskf"], writes=["maskf"])
        S.op("dve", lambda e: e.tensor_copy(ident[:], identf[:]), reads=["identf"], writes=["ident"])
        S.op("dve", lambda e: e.tensor_copy(mask2[:], maskf[:]), reads=["maskf"], writes=["mask2"])
        S.op("pool", lambda e: e.memset(onesbf[:], 1.0), writes=["onesbf"])
        S.dma("sp", hv_c[:], hv, writes=["hv_c"])
        S.dma("sp", invf_c[:], invf, writes=["invf_c"])

        def transposes(src_fn, n, banks, rkeys):
            def emit(e):
                last = None
                for i in range(n):
                    b = banks[i // 8]
                    last = e.transpose(psbf[b][:, (i % 8) * 128:(i % 8 + 1) * 128], src_fn(i), ident[:])
                return last
            S.op("pe", emit, reads=list(rkeys) + ["ident"], writes=[("ps", b) for b in banks[:(n + 7) // 8]])

        def layer_norm_multi(stk, items, post=None):
            bnst, mv, sd, rstd, nmr = stk

            def st_stats(i, it):
                xin = it["xin"]
                nchk = xin.shape[-1] // 512
                def e_stats(e):
                    last = None
                    for c in range(nchk):
                        last = e.bn_stats(bnst[:, i, c, :], xin[:, c * 512:(c + 1) * 512])
                    return last
                S.op("dve", e_stats, reads=[it["xkey"]], writes=[("bnst", i)])

            def st_aggr(i, it):
                nchk = it["xin"].shape[-1] // 512
                S.op("dve", lambda e: e.bn_aggr(mv[:, i, :], bnst[:, i, 0:nchk, :].rearrange("p a b -> p (a b)")),
                     reads=[("bnst", i)], writes=[("mv", i)])

            def st_sqrt(i, it):
                S.op("act", lambda e: e.activation(out=sd[:, i:i + 1], in_=mv[:, i, 1:2], func=AF.Sqrt,
                                                   bias=eps_c[:], scale=1.0),
                     reads=[("mv", i), "consts"], writes=[("sd", i)])

            def st_rstd(i, it):
                S.op("dve", lambda e: e.reciprocal(rstd[:, i:i + 1], sd[:, i:i + 1]),
                     reads=[("sd", i)], writes=[("rstd", i)])
                S.op("dve", lambda e: e.tensor_scalar(nmr[:, i:i + 1], mv[:, i, 0:1], -1.0, rstd[:, i:i + 1],
                                                      ALU.mult, ALU.mult),
                     reads=[("mv", i), ("rstd", i)], writes=[("nmr", i)])

            def st_norm(i, it):
                xin = it["xin"]
                S.op("act", lambda e: e.activation(out=xin, in_=xin, func=AF.Identity,
                                                   bias=nmr[:, i:i + 1], scale=rstd[:, i:i + 1]),
                     reads=[it["xkey"], ("rstd", i), ("nmr", i)], writes=[it["xkey"]])

            def st_gain(i, it):
                xin = it["xin"]
                wh = xin.shape[-1] // 2
                it["hkeys"] = [(it["xkey"], "lo"), (it["xkey"], "hi")]
                S.op("pool", lambda e: e.tensor_tensor(xin[:, 0:wh], xin[:, 0:wh], it["gt"][:, 0:wh], ALU.mult),
                     reads=[it["xkey"]] + it["tkeys"], writes=[it["hkeys"][0]])
                S.op("dve", lambda e: e.tensor_tensor(xin[:, wh:], xin[:, wh:], it["gt"][:, wh:], ALU.mult),
                     reads=[it["xkey"]] + it["tkeys"], writes=[it["hkeys"][1]])

            def st_bias(i, it):
                xin = it["xin"]
                if it.get("out_f32") is not None:
                    S.op("dve", lambda e: e.tensor_tensor(it["out_f32"], xin, it["bt"], ALU.add),
                         reads=[it["xkey"]] + it["hkeys"] + it["tkeys"], writes=[it["okey_f"]])
                elif it.get("out_bf") is not None:
                    S.op("dve", lambda e: e.tensor_tensor(it["out_bf"], xin, it["bt"], ALU.add),
                         reads=[it["xkey"]] + it["hkeys"] + it["tkeys"], writes=[it["okey_b"]])

            def st_cast(i, it):
                if it.get("out_f32") is not None and it.get("out_bf") is not None:
                    S.op("act", lambda e: e.activation(out=it["out_bf"], in_=it["out_f32"], func=AF.Copy),
                         reads=[it["okey_f"]], writes=[it["okey_b"]])

            def st_post(i, it):
                if post is not None:
                    post(i)

            stages = [st_stats, st_aggr, st_sqrt, st_rstd, st_norm, st_gain, st_bias, st_cast, st_post]
            n = len(items)
            for step in range(len(stages) + n - 1):
                for i in range(n):
                    k = step - i
                    if 0 <= k < len(stages):
                        stages[k](i, items[i])

        def layer_norm(stk, xin, xkey, gt, bt, tkeys, out_f32=None, okey_f=None, out_bf=None, okey_b=None):
            layer_norm_multi(stk, [dict(xin=xin, xkey=xkey, gt=gt, bt=bt, tkeys=tkeys, out_f32=out_f32,
                                        okey_f=okey_f, out_bf=out_bf, okey_b=okey_b)])

        def mm_group(bank_ap, pairs, rkeys, bank):
            n = len(pairs)
            def emit(e):
                last = None
                for i, (l, r) in enumerate(pairs):
                    last = e.matmul(bank_ap, l, r, start=(i == 0), stop=(i == n - 1))
                return last
            S.op("pe", emit, reads=rkeys, writes=[("ps", bank)])

        cjobs = []
        for name, K, N in WSPEC[1:]:
            for cg, row in enumerate(UNITS[name]):
                for (uid, k0, nkc) in row:
                    h1 = (nkc + 1) // 2
                    cjobs.append((name, uid, cg, k0, h1, 0))
                    cjobs.append((name, uid, cg, k0 + h1, nkc - h1, h1))
        cpos = {"next": 0}

        def make_conv(cstL, cbfL, tag, engines):
            n = len(cstL)
            st = {"loaded": [], "cast": [], "k": 0, "e": 0}

            def load_one():
                if cpos["next"] >= len(cjobs):
                    return
                job = cjobs[cpos["next"]]
                cpos["next"] += 1
                slot = st["k"] % n
                st["k"] += 1
                name, uid, cg, kr, nk, koff = job
                src = wd[name][kr * 128:(kr + nk) * 128, cg * 512:(cg + 1) * 512]
                S.dma("sp", cstL[slot][:, 0:nk, :], src.rearrange("(k p) c -> p k c", p=128),
                      writes=[(tag + "cst", slot)])
                st["loaded"].append((job, slot))

            def cast_one():
                if not st["loaded"]:
                    return
                job, slot = st["loaded"].pop(0)
                nk = job[4]
                eng = engines[st["e"] % len(engines)]
                st["e"] += 1
                if eng == "act":
                    S.op("act", lambda e: e.activation(out=cbfL[slot][:, 0:nk, :], in_=cstL[slot][:, 0:nk, :],
                                                       func=AF.Copy),
                         reads=[(tag + "cst", slot)], writes=[(tag + "cbf", slot)])
                else:
                    S.op(eng, lambda e: e.tensor_copy(cbfL[slot][:, 0:nk, :], cstL[slot][:, 0:nk, :]),
                         reads=[(tag + "cst", slot)], writes=[(tag + "cbf", slot)])
                st["cast"].append((job, slot))

            def store_one():
                if not st["cast"]:
                    return
                job, slot = st["cast"].pop(0)
                name, uid, cg, kr, nk, koff = job
                S.dma(STQ, wscr[uid, :, koff * 512:(koff + nk) * 512].rearrange("p (k c) -> p k c", c=512),
                      cbfL[slot][:, 0:nk, :], reads=[(tag + "cbf", slot)], writes=[("wscr", uid, koff)])

            def step():
                store_one()
                cast_one()
                load_one()

            def drain():
                while st["loaded"] or st["cast"]:
                    store_one()
                    cast_one()

            return step, drain, load_one

        try:
            with contextlib.ExitStack() as p0:
                cst = [sb(p0, "cst%d" % i, [128, 16, 512]) for i in range(3)]
                cbf = [sb(p0, "cbf%d" % i, [128, 16, 512], BF16) for i in range(3)]
                memt = sb(p0, "memt", [128, 2, D])
                membf = sb(p0, "membf", [128, 2, D], BF16)
                memT = sb(p0, "memT", [128, 16, MEM], BF16)
                KmemT = sb(p0, "KmemT0", [128, 16, MEM], BF16)
                Vmem = sb(p0, "Vmem0", [128, 2, D], BF16)
                cstate = {"n": 0}

                def convert(wname, k0, nkc, cg):
                    i = cstate["n"]
                    cstate["n"] += 1
                    slot = i % 3
                    src = wd[wname][k0 * 128:(k0 + nkc) * 128, cg * 512:(cg + 1) * 512]
                    S.dma("sp", cst[slot][:, 0:nkc, :], src.rearrange("(k p) c -> p k c", p=128),
                          writes=[("cst", slot)])
                    eng = ("act", "dve", "pool")[i % 3]
                    if eng == "act":
                        S.op("act", lambda e: e.activation(out=cbf[slot][:, 0:nkc, :], in_=cst[slot][:, 0:nkc, :],
                                                           func=AF.Copy),
                             reads=[("cst", slot)], writes=[("cbf", slot)])
                    else:
                        S.op(eng, lambda e: e.tensor_copy(cbf[slot][:, 0:nkc, :], cst[slot][:, 0:nkc, :]),
                             reads=[("cst", slot)], writes=[("cbf", slot)])
                    return slot

                S.dma("sp", memt[:], mem.rearrange("(t p) d -> p t d", p=128), writes=["memt"])
                S.op("dve", lambda e: e.tensor_copy(membf[:], memt[:]), reads=["memt"], writes=["membf"])
                for t in range(2):
                    transposes(lambda i, t=t: membf[:, t, i * 128:(i + 1) * 128], 16, [6, 7], ["membf"])
                    S.op("act", lambda e, t=t: e.activation(
                        out=memT[:, 0:8, t * 128:(t + 1) * 128],
                        in_=psbf[6].rearrange("p (k c) -> p k c", c=128), func=AF.Copy),
                        reads=[("ps", 6)], writes=["memT"])
                    S.op("dve", lambda e, t=t: e.tensor_copy(
                        memT[:, 8:16, t * 128:(t + 1) * 128],
                        psbf[7].rearrange("p (k c) -> p k c", c=128)),
                        reads=[("ps", 7)], writes=["memT"])
                for cg in range(4):
                    slot = convert("w_xk", 0, 16, cg)
                    for ec in range(4):
                        b = acc_bank()
                        mm_group(ps[b][:, 0:MEM],
                                 [(cbf[slot][:, kc, ec * 128:(ec + 1) * 128], memT[:, kc, :]) for kc in range(16)],
                                 [("cbf", slot), "memT"], b)
                        S.op("act", lambda e, b=b, cg=cg, ec=ec: e.activation(
                            out=KmemT[:, cg * 4 + ec, :], in_=ps[b][:, 0:MEM], func=AF.Copy),
                            reads=[("ps", b)], writes=["KmemT"])
                for cg in range(4):
                    slot = convert("w_xv", 0, 16, cg)
                    for t in range(2):
                        b = acc_bank()
                        mm_group(ps[b][:],
                                 [(memT[:, kc, t * 128:(t + 1) * 128], cbf[slot][:, kc, :]) for kc in range(16)],
                                 [("cbf", slot), "memT"], b)
                        S.op("dve", lambda e, b=b, cg=cg, t=t: e.tensor_copy(
                            Vmem[:, t, cg * 512:(cg + 1) * 512], ps[b][:]),
                            reads=[("ps", b)], writes=["Vmem"])
                S.dma("sp", KmemT_scr, KmemT[:].rearrange("p k m -> p (k m)"), reads=["KmemT"], writes=["KmemT_scr"])
                S.dma("sp", Vmem_scr, Vmem[:].rearrange("p t d -> p (t d)"), reads=["Vmem"], writes=["Vmem_scr"])
                for name, K, N in WSPEC[:1]:
                    for cg, row in enumerate(UNITS[name]):
                        for (uid, k0, nkc) in row:
                            slot = convert(name, k0, nkc, cg)
                            S.dma("sp", wscr[uid, :, 0:nkc * 512].rearrange("p (k c) -> p k c", c=512),
                                  cbf[slot][:, 0:nkc, :], reads=[("cbf", slot)], writes=[("wscr", uid)])
                S.barrier()
            if stop_after == '0':
                raise _Stop

            with contextlib.ExitStack() as pa:
              if 'A' not in DBG.get('skip', []):
                  lnG = sb(pa, "lnG", [128, D])
                  lnB = sb(pa, "lnB", [128, D])
                  sguG = sb(pa, "sguG", [128, 1024])
                  sguB = sb(pa, "sguB", [128, 1024])
                  gmG = sb(pa, "gmG", [128, 1024])
                  WsT = sb(pa, "WsT", [128, 8, 128], BF16)
                  bsT = sb(pa, "bsT", [128, 8])
                  xt = [sb(pa, "xt%d" % i, [128, D]) for i in range(4)]
                  hbf = [sb(pa, "hbf%d" % i, [128, D], BF16) for i in range(4)]
                  hT = sb(pa, "hT", [128, 16, TT], BF16)
                  posi = sb(pa, "posi", [128, TT], I32)
                  posf = sb(pa, "posf", [128, TT])
                  kf = sb(pa, "kf", [128, TT])
                  ki = posi
                  ang = sb(pa, "ang", [128, TT])
                  rabs = kf
                  sinS = sb(pa, "sinS", [128, TT])
                  cosT = sb(pa, "cosT", [128, TT])
                  xs = [sb(pa, "xs%d" % i, [128, TT]) for i in range(2)]
                  t1 = [sb(pa, "t1_%d" % i, [128, TT]) for i in range(2)]
                  qst = [sb(pa, "qst%d" % i, [128, TT], BF16) for i in range(3)]
                  vst = [sb(pa, "vst%d" % i, [128, 4, 129], BF16) for i in range(3)]
                  ugl = sb(pa, "ugl", [128, 4, 1024])
                  ggl = sb(pa, "ggl", [128, 4, 1024])
                  gln = sb(pa, "gln", [128, 1024], BF16)
                  gtmp = sb(pa, "gtmp", [128, 1024])
                  gm = gtmp
                  gmn = sb(pa, "gmn", [128, 1024], BF16)
                  gjunk = gmn
                  gmTs = [sb(pa, "gmTs%d" % i, [128, 8, 128], BF16) for i in range(2)]
                  lnstk = (sb(pa, "bnstA", [128, 4, 4, 6]), sb(pa, "mvA", [128, 4, 2]), sb(pa, "sdA", [128, 4]),
                           sb(pa, "rstdA", [128, 4]), sb(pa, "nmrA", [128, 4]))
                  ssq = sb(pa, "ssq", [128, 1])
                  sd2 = sb(pa, "sd2", [128, 1])
                  rs2 = sb(pa, "rs2", [128, 1])
                  accs["nb"] = 4
                  wr = [sb(pa, "wrA%d" % i, [128, 16, 512], BF16) for i in range(2)]
                  wstate["wr"] = wr
                  wstate["n"] = 0

                  S.dma("sp", lnG[:], tabs["ln_in_g"], writes=["tabA"])
                  S.dma("sp", lnB[:], tabs["ln_in_b"], writes=["tabA"])
                  S.dma("sp", sguG[:], tabs["sgu_norm_g"], writes=["tabA"])
                  S.dma("sp", sguB[:], tabs["sgu_norm_b"], writes=["tabA"])
                  S.dma("sp", gmG[:], tabs["gmlp_out_g"], writes=["tabA"])
                  WsN = ugl[:, 0, :].rearrange("p (g j) -> p g j", j=128)
                  WsNb = gln[:].rearrange("p (g j) -> p g j", j=128)
                  S.dma("sp", WsN, w_sp.rearrange("g i j -> i g j"), writes=["WsN", ("ugl", 0)])
                  with nc.allow_non_contiguous_dma(reason="tiny b_spatial transpose"):
                      S.dma("sp", bsT[:], b_sp.rearrange("g i -> i g"), writes=["bsT"])
                  S.op("dve", lambda e: e.tensor_tensor(
                      WsNb, WsN, maskf[:, 0:128].unsqueeze(1).to_broadcast([128, 8, 128]), ALU.mult),
                      reads=["WsN", "maskf", ("ugl", 0)], writes=["WsNb", ("gln", 0)])
                  transposes(lambda i: WsNb[:, i, :], 8, [6], ["WsNb", ("gln", 0)])
                  S.op("act", lambda e: e.activation(out=WsT[:], in_=psbf[6].rearrange("p (k c) -> p k c", c=128),
                                                     func=AF.Copy), reads=[("ps", 6)], writes=["WsT"])

                  gln2 = [gln, sb(pa, "gln_b", [128, 1024], BF16)]
                  gmn2 = [gmn, sb(pa, "gmn_b", [128, 1024], BF16)]
                  pend = {"t": None}

                  def gm_a(j):
                      layer_norm(lnstk, ggl[:, j, :], ("ggl", j), sguG[:], sguB[:], ["tabA"],
                                 out_bf=gln2[j % 2][:], okey_b=("gln", j % 2))

                  def gm_b(j):
                      gl = gln2[j % 2]
                      gn = gmn2[j % 2]
                      def e_sp(e):
                          last = None
                          for g in range(8):
                              last = e.matmul(ps[4 + g // 4][:, (g % 4) * 128:(g % 4 + 1) * 128],
                                              WsT[:, g, :], gl[:, g * 128:(g + 1) * 128], start=True, stop=True)
                          return last
                      S.op("pe", e_sp, reads=["WsT", ("gln", j % 2)], writes=[("ps", 4), ("ps", 5)])
                      for hb in range(2):
                          S.op("dve", lambda e, hb=hb: e.tensor_tensor(
                              gtmp[:, hb * 512:(hb + 1) * 512].rearrange("p (g e) -> p g e", e=128),
                              ps[4 + hb][:].rearrange("p (g e) -> p g e", e=128),
                              bsT[:, hb * 4:(hb + 1) * 4].unsqueeze(2).to_broadcast([128, 4, 128]), ALU.add),
                              reads=[("ps", 4 + hb), "bsT"], writes=[("gtmp", hb)])
                      S.op("pool", lambda e: e.tensor_tensor(gm[:], gtmp[:], ugl[:, j, :], ALU.mult),
                           reads=[("gtmp", 0), ("gtmp", 1), ("ugl", j)], writes=["gm", ("gtmp", 0), ("gtmp", 1)])
                      S.op("pool", lambda e: e.memset(ssq[:], 0.0), writes=["ssq"])
                      S.op("act", lambda e: e.activation(out=gn[:], in_=gm[:], func=AF.Square, accum_out=ssq[:]),
                           reads=["gm", "ssq"], writes=["ssq", ("gmn", j % 2)])
                      S.op("act", lambda e: e.activation(out=sd2[:], in_=ssq[:], func=AF.Sqrt, bias=eps_c[:],
                                                         scale=1.0 / 1024),
                           reads=["ssq", "consts"], writes=["sd2"])
                      S.op("dve", lambda e: e.reciprocal(rs2[:], sd2[:]), reads=["sd2"], writes=["rs2"])
                      S.op("dve", lambda e: e.scalar_tensor_tensor(gn[:], gm[:], rs2[:], gmG[:], ALU.mult, ALU.mult),
                           reads=["gm", "rs2", "tabA"], writes=[("gmn", j % 2)])

                  def gm_c(c_own, j):
                      gn = gmn2[j % 2]
                      transposes(lambda i: gn[:, i * 128:(i + 1) * 128], 8, [6], [("gmn", j % 2)])
                      g2 = j % 2
                      S.op("act", lambda e: e.activation(
                          out=gmTs[g2][:], in_=psbf[6].rearrange("p (k c) -> p k c", c=128), func=AF.Copy),
                          reads=[("ps", 6)], writes=[("gmTs", g2)])
                      S.dma(STQ, gmT_scr[:, :, c_own * 128:(c_own + 1) * 128], gmTs[g2][:],
                            reads=[("gmTs", g2)], writes=[("gmT_scr", c_own)])

                  hoisted = {}

                  def ln_pair(tt, jp):
                      halo_ = tt < THALO // TT
                      items = []
                      sls = []
                      for j in (2 * jp, 2 * jp + 1):
                          c = tt * 4 + j
                          sl = xslot["n"] % 4
                          xslot["n"] += 1
                          sls.append((j, sl))
                          S.dma("sp", xt[sl][:], x[c * 128:(c + 1) * 128, :], writes=[("xt", sl)])
                          if halo_:
                              items.append(dict(xin=xt[sl][:], xkey=("xt", sl), gt=lnG[:], bt=lnB[:], tkeys=["tabA"],
                                                out_bf=hbf[sl][:], okey_b=("hbf", sl)))
                          else:
                              items.append(dict(xin=xt[sl][:], xkey=("xt", sl), gt=lnG[:], bt=lnB[:], tkeys=["tabA"],
                                                out_f32=xt[sl][:], okey_f=("xt", sl),
                                                out_bf=hbf[sl][:], okey_b=("hbf", sl)))
                      layer_norm_multi(lnstk, items)
                      if not halo_:
                          for (j, sl) in sls:
                              co = (tt - THALO // TT) * 4 + j
                              S.dma(STQ, h_scr[co * 128:(co + 1) * 128, :], xt[sl][:],
                                    reads=[("xt", sl)], writes=[("h_scr", co)])
                      return sls

                  def tr_pair(sls):
                      for (j, sl) in sls:
                          transposes(lambda i, sl=sl: hbf[sl][:, i * 128:(i + 1) * 128], 16, [6, 7], [("hbf", sl)])
                          S.op("act", lambda e, j=j: e.activation(
                              out=hT[:, 0:8, j * 128:(j + 1) * 128],
                              in_=psbf[6].rearrange("p (k c) -> p k c", c=128), func=AF.Copy),
                              reads=[("ps", 6)], writes=[("hT", j)])
                          S.op("dve", lambda e, j=j: e.tensor_copy(
                              hT[:, 8:16, j * 128:(j + 1) * 128],
                              psbf[7].rearrange("p (k c) -> p k c", c=128)),
                              reads=[("ps", 7)], writes=[("hT", j)])

                  qslot = {"n": 0}
                  vslot = {"n": 0}
                  xslot = {"n": 0}
                  for t in DBG.get('tilesA', range(TTOT // TT)):
                      halo = t < THALO // TT
                      tok0 = t * TT
                      sls0 = hoisted.pop((t, 0)) if (t, 0) in hoisted else ln_pair(t, 0)
                      tr_pair(sls0)
                      sls1 = hoisted.pop((t, 1)) if (t, 1) in hoisted else ln_pair(t, 1)
                      tr_pair(sls1)
                      hT_all = [("hT", j) for j in range(4)]
                      S.dma("sp", posi[:], pos[:, tok0:tok0 + TT], writes=["posi"])
                      S.op("dve", lambda e: e.tensor_copy(posf[:], posi[:]), reads=["posi"], writes=["posf"])
                      S.op("dve", lambda e: e.tensor_scalar(kf[:], posf[:], invf_c[:], INV_2PI, ALU.mult, ALU.mult),
                           reads=["posf", "invf_c"], writes=["kf"])
                      S.op("dve", lambda e: e.tensor_copy(ki[:], kf[:]), reads=["kf", "posf"], writes=["posi"])
                      S.op("dve", lambda e: e.tensor_copy(kf[:], ki[:]), reads=["posi"], writes=["kf"])
                      S.op("dve", lambda e: e.tensor_scalar(ang[:], posf[:], invf_c[:], None, ALU.mult),
                           reads=["posf", "invf_c"], writes=["ang"])
                      S.op("dve", lambda e: e.scalar_tensor_tensor(ang[:], kf[:], -CW1, ang[:], ALU.mult, ALU.add),
                           reads=["kf", "ang"], writes=["ang"])
                      S.op("dve", lambda e: e.scalar_tensor_tensor(ang[:], kf[:], -CW2, ang[:], ALU.mult, ALU.add),
                           reads=["kf", "ang"], writes=["ang"])
                      S.op("dve", lambda e: e.tensor_scalar(ang[:], ang[:], PI, -PI, ALU.min, ALU.max),
                           reads=["ang"], writes=["ang"])
                      S.op("act", lambda e: e.activation(out=rabs[:], in_=ang[:], func=AF.Abs),
                           reads=["ang"], writes=["kf"])
                      S.op("act", lambda e: e.activation(out=sinS[:], in_=ang[:], func=AF.Sin, scale=sgn_c[:]),
                           reads=["ang", "consts"], writes=["sinS"])
                      S.op("act", lambda e: e.activation(out=cosT[:], in_=rabs[:], func=AF.Sin, scale=-1.0, bias=hpi_c[:]),
                           reads=["kf", "consts"], writes=["cosT"])
                      ulist = [2, 3, 4, 5] if halo else list(range(10))
                      ulist = [u for u in ulist if u in DBG.get('units', ulist)]
                      for ui_, u in enumerate(ulist):
                          slot = wload(UNITS["w_in"][u][0])
                          if u < 4:
                              for hh in range(4):
                                  head = (u % 2) * 4 + hh
                                  b = acc_bank()
                                  mm_group(ps[b][:], [(wr[slot][:, kc, hh * 128:(hh + 1) * 128], hT[:, kc, :])
                                                      for kc in range(16)], [("wr", slot)] + hT_all, b)
                                  r = qslot["n"] % 2
                                  q3 = qslot["n"] % 3
                                  qslot["n"] += 1
                                  def e_rope(e, r=r, b=b):
                                      e.tensor_tensor(t1[r][0:64, :], ps[b][64:128, :], sinS[0:64, :], ALU.mult)
                                      e.tensor_tensor(t1[r][64:128, :], ps[b][0:64, :], sinS[64:128, :], ALU.mult)
                                      return e.tensor_tensor(xs[r][:], ps[b][:], cosT[:], ALU.mult)
                                  S.op("dve", e_rope, reads=[("ps", b), "sinS", "cosT"], writes=[("t1", r), ("xs", r)])
                                  S.op("pool", lambda e, r=r, q3=q3: e.tensor_tensor(qst[q3][:], t1[r][:], xs[r][:], ALU.add),
                                       reads=[("t1", r), ("xs", r)], writes=[("qst", q3)])
                                  if u < 2:
                                      dst = qT_scr[head, :, tok0 - THALO:tok0 - THALO + TT]
                                      dk = ("qT_scr", head, t)
                                  else:
                                      dst = kT_scr[head, :, tok0:tok0 + TT]
                                      dk = ("kT_scr", head, t)
                                  if DBG.get('qk_dma', True):
                                      S.dma(STQ, dst, qst[q3][:], reads=[("qst", q3)], writes=[dk])
                          else:
                              for j in range(4):
                                  c = t * 4 + j
                                  b = acc_bank()
                                  mm_group(ps[b][:], [(hT[:, kc, j * 128:(j + 1) * 128], wr[slot][:, kc, :])
                                                      for kc in range(16)], [("wr", slot), ("hT", j)], b)
                                  if u < 6:
                                      v3 = vslot["n"] % 3
                                      vslot["n"] += 1
                                      vc = hv_c if halo else one_c
                                      S.op("act", lambda e, b=b, v3=v3, vc=vc: e.activation(
                                          out=vst[v3][:, :, 0:128], in_=ps[b][:].rearrange("p (h e) -> p h e", e=128),
                                          func=AF.Copy, scale=vc[:]),
                                          reads=[("ps", b), "hv_c", "consts"], writes=[("vst", v3)])
                                      S.op("pool", lambda e, v3=v3, vc=vc: e.tensor_copy(
                                          vst[v3][:, :, 128:129], vc[:].unsqueeze(1).to_broadcast([128, 4, 1])),
                                          reads=["hv_c", "consts"], writes=[("vst", v3)])
                                      S.dma(STQ, v_scr[c * 128:(c + 1) * 128, (u - 4) * 516:(u - 3) * 516],
                                            vst[v3][:].rearrange("p h e -> p (h e)"),
                                            reads=[("vst", v3)], writes=[("v_scr", c, u)])
                                  elif u < 8:
                                      S.op("act", lambda e, b=b, j=j, u=u: e.activation(
                                          out=ugl[:, j, (u - 6) * 512:(u - 5) * 512], in_=ps[b][:], func=AF.Gelu),
                                          reads=[("ps", b)], writes=[("ugl", j)])
                                  else:
                                      S.op("act", lambda e, b=b, j=j, u=u: e.activation(
                                          out=ggl[:, j, (u - 8) * 512:(u - 7) * 512], in_=ps[b][:], func=AF.Gelu),
                                          reads=[("ps", b)], writes=[("ggl", j)])
                          if ui_ in (1, 2) and 'tilesA' not in DBG and t + 1 < TTOT // TT:
                              hoisted[(t + 1, ui_ - 1)] = ln_pair(t + 1, ui_ - 1)
                          if pend["t"] is not None and not halo:
                              tp_ = pend["t"]
                              if 0 <= ui_ - 2 < 4:
                                  gm_c((tp_ - THALO // TT) * 4 + ui_ - 2, ui_ - 2)
                              if 0 <= ui_ - 1 < 4:
                                  gm_b(ui_ - 1)
                              if 0 <= ui_ < 4:
                                  gm_a(ui_)
                      if halo or not DBG.get('gmlp', True):
                          continue
                      pend["t"] = t
                  if pend["t"] is not None:
                      for j in range(4):
                          gm_a(j)
                          gm_b(j)
                          gm_c((pend["t"] - THALO // TT) * 4 + j, j)
                  S.barrier()
            if stop_after == 'A':
                raise _Stop

            with contextlib.ExitStack() as pb:
              if 'B' not in DBG.get('skip', []):
                  KT = sb(pb, "KT", [128, NH, 4096], BF16)
                  QT = sb(pb, "QT", [128, NH, 2048], BF16)
                  Vb = [sb(pb, "Vb%d" % i, [128, 2, VW], BF16) for i in range(4)]
                  PT = [sb(pb, "PT%d" % i, [128, 4, 256], BF16) for i in range(2)]
                  stg = [sb(pb, "stg%d" % i, [128, NH, 129]) for i in range(2)]
                  blk = {"n": 0, "g": 0}
                  scale_qk = float(128.0 ** -0.5)
                  cstB = [sb(pb, "cstB%d" % i, [128, 8, 512]) for i in range(3)]
                  cbfB = [sb(pb, "cbfB%d" % i, [128, 8, 512], BF16) for i in range(3)]
                  cstep, cdrain, cload = make_conv(cstB, cbfB, "B", ["dve"])
                  cload()
                  cload()
                  for sp_ in range(2):
                      for h in range(NH):
                          S.dma("sp", KT[:, h, :], kT_scr[h, :, sp_ * 2048:sp_ * 2048 + 4096],
                                writes=[("KT", h)])
                          S.dma("sp", QT[:, h, :], qT_scr[h, :, sp_ * 2048:(sp_ + 1) * 2048],
                                writes=[("QT", h)])
                      for dix, d in enumerate(BRANCH_D):
                          for r in range(d):
                              for qb in range(16 // d):
                                  n = blk["n"]
                                  blk["n"] += 1
                                  vs = n % 4
                                  ss = n % 2
                                  q0 = r + qb * 128 * d
                                  span_ = 127 * d + 1
                                  qsl = slice(q0, q0 + span_, d)
                                  kc0 = 2048 + q0
                                  ksl = (slice(kc0 - 128 * d, kc0 - 128 * d + span_, d), slice(kc0, kc0 + span_, d))
                                  tk = sp_ * 2048 + kc0
                                  S.dma("sp", Vb[vs][:, 0, :], v_scr[tk - 128 * d:tk - 128 * d + span_:d, :], writes=[("Vb", vs)])
                                  S.dma("sp", Vb[vs][:, 1, :], v_scr[tk:tk + span_:d, :], writes=[("Vb", vs)])
                                  cstep()
                                  for hg in range(2):
                                      g = blk["g"]
                                      blk["g"] += 1
                                      sbk = (0, 1) if g % 2 == 0 else (2, 3)
                                      obk = (4, 5) if g % 2 == 0 else (6, 7)
                                      pt = PT[g % 2]
                                      def e_qk(e, hg=hg, sbk=sbk, ksl=ksl, qsl=qsl):
                                          last = None
                                          for hh in range(4):
                                              h = hg * 4 + hh
                                              for ti in range(2):
                                                  o = (hh % 2) * 256 + ti * 128
                                                  last = e.matmul(ps[sbk[hh // 2]][:, o:o + 128], KT[:, h, ksl[ti]],
                                                                  QT[:, h, qsl], start=True, stop=True)
                                          return last
                                      S.op("pe", e_qk, reads=[("KT", h) for h in range(hg * 4, hg * 4 + 4)] +
                                           [("QT", h) for h in range(hg * 4, hg * 4 + 4)],
                                           writes=[("ps", sbk[0]), ("ps", sbk[1])])
                                      for bi in range(2):
                                          S.op("act", lambda e, bi=bi, pt=pt, sbk=sbk: e.activation(
                                              out=pt[:, bi * 2:bi * 2 + 2, :].rearrange("p h k -> p (h k)"),
                                              in_=ps[sbk[bi]][:], func=AF.Exp, scale=scale_qk),
                                              reads=[("ps", sbk[bi])], writes=[("PT", g % 2, bi)])
                                      S.op("pool", lambda e, pt=pt: e.tensor_tensor(
                                          pt[:], pt[:], mask2[:].unsqueeze(1).to_broadcast([128, 4, 256]), ALU.mult),
                                          reads=[("PT", g % 2, 0), ("PT", g % 2, 1), "mask2"],
                                          writes=[("PT", g % 2, 0), ("PT", g % 2, 1)])
                                      o0 = sp_ * 2048 + q0
                                      def tail(hg=hg, obk=obk, pt=pt, vs=vs, ss=ss, g=g, dix=dix, o0=o0, span_=span_, d=d,
                                               sp_=sp_, r=r, qb=qb):
                                          def e_pv(e):
                                              last = None
                                              for hh in range(4):
                                                  h = hg * 4 + hh
                                                  o = (hh % 2) * 256
                                                  for ti in range(2):
                                                      last = e.matmul(ps[obk[hh // 2]][:, o:o + 129],
                                                                      pt[:, hh, ti * 128:(ti + 1) * 128],
                                                                      Vb[vs][:, ti, h * 129:(h + 1) * 129],
                                                                      start=(ti == 0), stop=(ti == 1))
                                              return last
                                          S.op("pe", e_pv, reads=[("PT", g % 2, 0), ("PT", g % 2, 1), ("Vb", vs)],
                                               writes=[("ps", obk[0]), ("ps", obk[1])])
                                          for bi in range(2):
                                              S.op("dve", lambda e, bi=bi: e.tensor_copy(
                                                  stg[ss][:, hg * 4 + bi * 2:hg * 4 + bi * 2 + 2, :],
                                                  ps[obk[bi]][:].rearrange("p (h k) -> p h k", k=256)[:, :, 0:129]),
                                                  reads=[("ps", obk[bi])], writes=[("stg", ss)])
                                          if hg == 1:
                                              S.dma(STQ, attn_scr[dix, o0:o0 + span_:d, :],
                                                    stg[ss][:].rearrange("p h k -> p (h k)"),
                                                    reads=[("stg", ss)], writes=[("attn_scr", dix, sp_, r, qb)])
                                      if blk.get("tail") is not None:
                                          blk["tail"]()
                                      blk["tail"] = tail
                  blk["tail"]()
                  cdrain()
                  S.barrier()
            if stop_after == 'B':
                raise _Stop

            with contextlib.ExitStack() as pm:
              if 'B2' not in DBG.get('skip', []):
                  aG = sb(pm, "aG", [128, 1024])
                  av = [[sb(pm, "av%d_%d" % (i, k), [128, NH, 129]) for k in range(3)] for i in range(2)]
                  rden = sb(pm, "rden", [128, NH])
                  at = sb(pm, "at", [128, 1024])
                  ajunk = sb(pm, "ajunk", [128, 1024], BF16)
                  atn = sb(pm, "atn", [128, 1024], BF16)
                  aTs = [sb(pm, "aTs%d" % i, [128, 8, 128], BF16) for i in range(2)]
                  ssq = sb(pm, "ssqm", [128, 1])
                  sd2 = sb(pm, "sd2m", [128, 1])
                  rs2 = sb(pm, "rs2m", [128, 1])
                  S.dma("sp", aG[:], tabs["attn_out_g"], writes=["aG"])
                  cstM = [sb(pm, "cstM%d" % i, [128, 8, 512]) for i in range(3)]
                  cbfM = [sb(pm, "cbfM%d" % i, [128, 8, 512], BF16) for i in range(3)]
                  cstep, cdrain, cload = make_conv(cstM, cbfM, "M", ["act", "pool"])
                  cload()
                  cload()
                  for c in range(TOWN // 128):
                      s2 = c % 2
                      cstep()
                      cstep()
                      for k in range(3):
                          S.dma("sp", av[s2][k][:].rearrange("p h k -> p (h k)"), attn_scr[k, c * 128:(c + 1) * 128, :],
                                writes=[("av", s2, k)])
                      S.op("dve", lambda e, s2=s2: e.tensor_tensor(av[s2][0][:], av[s2][0][:], av[s2][1][:], ALU.add),
                           reads=[("av", s2, 0), ("av", s2, 1)], writes=[("av", s2, 0)])
                      S.op("dve", lambda e, s2=s2: e.tensor_tensor(av[s2][0][:], av[s2][0][:], av[s2][2][:], ALU.add),
                           reads=[("av", s2, 0), ("av", s2, 2)], writes=[("av", s2, 0)])
                      S.op("dve", lambda e, s2=s2: e.reciprocal(rden[:].unsqueeze(2), av[s2][0][:, :, 128:129]),
                           reads=[("av", s2, 0)], writes=["rden"])
                      S.op("dve", lambda e, s2=s2: e.tensor_tensor(
                          at[:].rearrange("p (h e) -> p h e", e=128), av[s2][0][:, :, 0:128],
                          rden[:].unsqueeze(2).to_broadcast([128, NH, 128]), ALU.mult),
                          reads=[("av", s2, 0), "rden"], writes=["at"])
                      S.op("pool", lambda e: e.memset(ssq[:], 0.0), writes=["ssqm"])
                      S.op("act", lambda e: e.activation(out=ajunk[:], in_=at[:], func=AF.Square, accum_out=ssq[:]),
                           reads=["at", "ssqm"], writes=["ssqm", "ajunk"])
                      S.op("act", lambda e: e.activation(out=sd2[:], in_=ssq[:], func=AF.Sqrt, bias=eps_c[:],
                                                         scale=1.0 / 1024), reads=["ssqm"], writes=["sd2m"])
                      S.op("dve", lambda e: e.reciprocal(rs2[:], sd2[:]), reads=["sd2m"], writes=["rs2m"])
                      S.op("dve", lambda e: e.scalar_tensor_tensor(atn[:], at[:], rs2[:], aG[:], ALU.mult, ALU.mult),
                           reads=["at", "rs2m", "aG"], writes=["atn"])
                      transposes(lambda i: atn[:, i * 128:(i + 1) * 128], 8, [6 + s2], ["atn"])
                      S.op("act", lambda e, s2=s2: e.activation(
                          out=aTs[s2][:], in_=psbf[6 + s2].rearrange("p (k c) -> p k c", c=128), func=AF.Copy),
                          reads=[("ps", 6 + s2)], writes=[("aTs", s2)])
                      S.dma(STQ, attnT_scr[:, :, c * 128:(c + 1) * 128], aTs[s2][:],
                            reads=[("aTs", s2)], writes=[("attnT_scr", c)])
                  while cpos["next"] < len(cjobs):
                      cstep()
                  cdrain()
                  S.barrier()
            if stop_after == 'B2':
                raise _Stop

            with contextlib.ExitStack() as pc:
                res = sb(pc, "res", [128, 4, D])
                hbf = [sb(pc, "hbfC%d" % i, [128, D], BF16) for i in range(4)]
                TB = [sb(pc, "TB%d" % i, [128, 16, TT], BF16) for i in range(2)]
                actT = sb(pc, "actT", [128, 24, TT], BF16)
                tabG = sb(pc, "tabG", [128, D])
                tabB = sb(pc, "tabB", [128, D])
                PTx = [sb(pc, "PTx%d" % i, [128, 2, TT], BF16) for i in range(2)]
                rdx = [sb(pc, "rdx%d" % i, [128, TT]) for i in range(2)]
                sg = [sb(pc, "sg%d" % i, [128, TT]) for i in range(2)]
                lnstk = (sb(pc, "bnstC", [128, 4, 4, 6]), sb(pc, "mvC", [128, 4, 2]), sb(pc, "sdC", [128, 4]),
                         sb(pc, "rstdC", [128, 4]), sb(pc, "nmrC", [128, 4]))
                accs["nb"] = 6
                wr = [sb(pc, "wrC%d" % i, [128, 16, 512], BF16) for i in range(3)]
                wstate["wr"] = wr
                wstate["n"] = 0
                KmemT = sb(pc, "KmemT", [128, 16, MEM], BF16)
                Vmem = sb(pc, "Vmem", [128, 2, D], BF16)
                S.dma("sp", KmemT[:].rearrange("p k m -> p (k m)"), KmemT_scr, writes=["KmemT"])
                S.dma("sp", Vmem[:].rearrange("p t d -> p (t d)"), Vmem_scr, writes=["Vmem"])
                scale_x = float(512.0 ** -0.5)
                cnt = {"pt": 0, "sg": 0, "hb": 0}

                def load_tabs(gname, bname):
                    S.dma("sp", tabG[:], tabs[gname], writes=["tabC"])
                    S.dma("sp", tabB[:], tabs[bname], writes=["tabC"])

                def ln_and_transpose(dstT, dkey, want_f32_keep=True):
                    items = [dict(xin=res[:, j, :], xkey=("res", j), gt=tabG[:], bt=tabB[:], tkeys=["tabC"],
                                  out_f32=res[:, j, :], okey_f=("res", j), out_bf=hbf[j][:], okey_b=("hbfC", j))
                             for j in range(4)]

                    def post(j):
                        bk = [6, 7] if j % 2 == 0 else [4, 5]
                        transposes(lambda i: hbf[j][:, i * 128:(i + 1) * 128], 16, bk, [("hbfC", j)])
                        S.op("act", lambda e: e.activation(
                            out=dstT[:, 0:8, j * 128:(j + 1) * 128],
                            in_=psbf[bk[0]].rearrange("p (k c) -> p k c", c=128), func=AF.Copy),
                            reads=[("ps", bk[0])], writes=[(dkey, j)])
                        S.op("dve", lambda e: e.tensor_copy(
                            dstT[:, 8:16, j * 128:(j + 1) * 128],
                            psbf[bk[1]].rearrange("p (k c) -> p k c", c=128)),
                            reads=[("ps", bk[1])], writes=[(dkey, j)])
                    layer_norm_multi(lnstk, items, post=post)

                def ln_plain():
                    items = [dict(xin=res[:, j, :], xkey=("res", j), gt=tabG[:], bt=tabB[:], tkeys=["tabC"],
                                  out_f32=res[:, j, :], okey_f=("res", j)) for j in range(4)]
                    layer_norm_multi(lnstk, items)

                def proj_tokmajor(srcT, skey, wname, first=True):
                    def grp(cg, slot, j):
                        b = acc_bank()
                        mm_group(ps[b][:], [(srcT[:, kc, j * 128:(j + 1) * 128], wr[slot][:, kc, :])
                                            for kc in range(16)], [("wr", slot), (skey, j)], b)
                        S.op("dve", lambda e: e.scalar_tensor_tensor(
                            res[:, j, cg * 512:(cg + 1) * 512], res[:, j, cg * 512:(cg + 1) * 512], ALPHA,
                            ps[b][:], ALU.mult, ALU.add),
                            reads=[("ps", b), ("res", j)], writes=[("res", j)])
                    for cg in range(2):
                        slot = wload(UNITS[wname][cg][0])
                        for j in range(4):
                            grp(cg, slot, j)
                    slot2 = wload(UNITS[wname][2][0])
                    slot3 = wload(UNITS[wname][3][0])
                    for j in range(4):
                        grp(2, slot2, j)
                        grp(3, slot3, j)

                mixT_done = set()

                def load_mixT(tt):
                    oo = tt * TT
                    S.dma("sp", TB[0][:, 0:8, :], attnT_scr[:, :, oo:oo + TT], writes=[("TA", j) for j in range(4)])
                    S.dma("sp", TB[0][:, 8:16, :], gmT_scr[:, :, oo:oo + TT], writes=[("TA", j) for j in range(4)])
                    mixT_done.add(tt)

                for t in DBG.get('tilesC', range(TOWN // TT)):
                    o0 = t * TT
                    A_, B_ = TB[0], TB[1]
                    if t not in mixT_done:
                        load_mixT(t)
                    for j in range(4):
                        c = o0 // 128 + j
                        S.dma("sp", res[:, j, :], h_scr[c * 128:(c + 1) * 128, :], writes=[("res", j)])
                    proj_tokmajor(A_, "TA", "w_mix")
                    if DBG.get('stepC', 99) <= 1:
                        continue
                    load_tabs("ln1_g", "ln1_b")
                    ln_and_transpose(B_, "TB")
                    if DBG.get('stepC', 99) <= 2:
                        continue
                    for cg in range(4):
                        slot = wload(UNITS["w_xq"][cg][0])
                        for ec in range(4):
                            b = acc_bank()
                            mm_group(ps[b][:], [(wr[slot][:, kc, ec * 128:(ec + 1) * 128], B_[:, kc, :])
                                                for kc in range(16)], [("wr", slot)] + [("TB", j) for j in range(4)], b)
                            S.op("act", lambda e, b=b, cg=cg, ec=ec: e.activation(
                                out=A_[:, cg * 4 + ec, :], in_=ps[b][:], func=AF.Copy),
                                reads=[("ps", b)], writes=[("TAq", cg * 4 + ec)] + [("TA", j) for j in range(4)])
                    TAall = [("TA", j) for j in range(4)]
                    if DBG.get('stepC', 99) <= 3:
                        continue
                    for hx in range(4):
                        pslot = cnt["pt"] % 2
                        cnt["pt"] += 1
                        for mt in range(2):
                            b = acc_bank()
                            mm_group(ps[b][:], [(KmemT[:, hx * 4 + ec, mt * 128:(mt + 1) * 128], A_[:, hx * 4 + ec, :])
                                                for ec in range(4)], ["KmemT"] + TAall, b)
                            S.op("act", lambda e, b=b, mt=mt, pslot=pslot: e.activation(
                                out=PTx[pslot][:, mt, :], in_=ps[b][:], func=AF.Exp, scale=scale_x),
                                reads=[("ps", b)], writes=[("PTx", pslot)])
                        if DBG.get('attnC', 9) <= 1:
                            continue
                        b = acc_bank()
                        mm_group(ps[b][:], [(onesbf[:], PTx[pslot][:, mt, :]) for mt in range(2)],
                                 [("PTx", pslot), "onesbf"], b)
                        S.op("dve", lambda e, b=b, pslot=pslot: e.reciprocal(rdx[pslot][:], ps[b][:]),
                             reads=[("ps", b)], writes=[("rdx", pslot)])
                        if DBG.get('attnC', 9) <= 2:
                            continue
                        for ec in range(4):
                            b = acc_bank()
                            col = hx * 512 + ec * 128
                            mm_group(ps[b][:], [(Vmem[:, mt, col:col + 128], PTx[pslot][:, mt, :]) for mt in range(2)],
                                     [("PTx", pslot), "Vmem"], b)
                            if DBG.get('pvcopy', False):
                                S.op("dve", lambda e, b=b, pslot=pslot, hx=hx, ec=ec: e.tensor_copy(
                                    B_[:, hx * 4 + ec, :], ps[b][:]),
                                    reads=[("ps", b), ("rdx", pslot)], writes=[("TB", j) for j in range(4)])
                                continue
                            if DBG.get('pvswap', True):
                                S.op("dve", lambda e, b=b, pslot=pslot, hx=hx, ec=ec: e.tensor_tensor(
                                    B_[:, hx * 4 + ec, :], rdx[pslot][:], ps[b][:], ALU.mult),
                                    reads=[("ps", b), ("rdx", pslot)], writes=[("TB", j) for j in range(4)])
                                continue
                            S.op("dve", lambda e, b=b, pslot=pslot, hx=hx, ec=ec: e.tensor_tensor(
                                B_[:, hx * 4 + ec, :], ps[b][:], rdx[pslot][:], ALU.mult),
                                reads=[("ps", b), ("rdx", pslot)], writes=[("TB", j) for j in range(4)])
                    if DBG.get('stepC', 99) <= 4:
                        continue
                    proj_tokmajor(B_, "TB", "w_xo")
                    if DBG.get('stepC', 99) <= 5:
                        continue
                    load_tabs("ln2_g", "ln2_b")
                    ln_and_transpose(A_, "TA")
                    if DBG.get('stepC', 99) <= 6:
                        continue
                    for half, (u0, u1) in enumerate(((0, 6), (6, 11))):
                        for u in range(u0, u1):
                            sg_slot = wload(UNITS["w_gate"][u][0])
                            su_slot = wload(UNITS["w_up"][u][0])
                            for fi in range(4):
                                fl = (u - u0) * 4 + fi
                                bg = acc_bank()
                                mm_group(ps[bg][:], [(wr[sg_slot][:, kc, fi * 128:(fi + 1) * 128], A_[:, kc, :])
                                                     for kc in range(16)], [("wr", sg_slot)] + TAall, bg)
                                bu = acc_bank()
                                mm_group(ps[bu][:], [(wr[su_slot][:, kc, fi * 128:(fi + 1) * 128], A_[:, kc, :])
                                                     for kc in range(16)], [("wr", su_slot)] + TAall, bu)
                                s_ = cnt["sg"] % 2
                                cnt["sg"] += 1
                                S.op("act", lambda e, bg=bg, s_=s_: e.activation(out=sg[s_][:], in_=ps[bg][:], func=AF.Silu),
                                     reads=[("ps", bg)], writes=[("sg", s_)])
                                S.op("dve", lambda e, bu=bu, s_=s_, fl=fl: e.tensor_tensor(
                                    actT[:, fl, :], sg[s_][:], ps[bu][:], ALU.mult),
                                    reads=[("ps", bu), ("sg", s_)], writes=[("actT", fl)])
                        nfl = (u1 - u0) * 4
                        for cgo in range(4):
                            banks = [acc_bank() for _ in range(4)]
                            kus = UNITS["w_down"][cgo][half * 2:half * 2 + 2]
                            fl0 = 0
                            for ui, unit in enumerate(kus):
                                slot = wload(unit)
                                nkc = unit[2]
                                for j in range(4):
                                    def e_dn(e, j=j, slot=slot, nkc=nkc, fl0=fl0, ui=ui, b=banks[j]):
                                        last = None
                                        for kc in range(nkc):
                                            last = e.matmul(ps[b][:], actT[:, fl0 + kc, j * 128:(j + 1) * 128],
                                                            wr[slot][:, kc, :], start=(ui == 0 and kc == 0),
                                                            stop=(ui == 1 and kc == nkc - 1))
                                        return last
                                    S.op("pe", e_dn, reads=[("wr", slot)] + [("actT", fl0 + kc) for kc in range(nkc)],
                                         writes=[("ps", banks[j])])
                                fl0 += nkc
                            assert fl0 == nfl
                            for j in range(4):
                                b = banks[j]
                                if half == 0:
                                    S.op("dve", lambda e, b=b, j=j, cgo=cgo: e.scalar_tensor_tensor(
                                        res[:, j, cgo * 512:(cgo + 1) * 512], res[:, j, cgo * 512:(cgo + 1) * 512],
                                        ALPHA, ps[b][:], ALU.mult, ALU.add),
                                        reads=[("ps", b), ("res", j)], writes=[("res", j)])
                                else:
                                    S.op("dve", lambda e, b=b, j=j, cgo=cgo: e.tensor_tensor(
                                        res[:, j, cgo * 512:(cgo + 1) * 512], res[:, j, cgo * 512:(cgo + 1) * 512],
                                        ps[b][:], ALU.add),
                                        reads=[("ps", b), ("res", j)], writes=[("res", j)])
                    if DBG.get('stepC', 99) <= 7:
                        continue
                    if t + 1 < TOWN // TT and 'tilesC' not in DBG:
                        load_mixT(t + 1)
                        for cgp in range(3):
                            wprefetch(UNITS["w_mix"][cgp][0])
                    load_tabs("ln3_g", "ln3_b")
                    ln_plain()
                    for j in range(4):
                        S.dma(STQ, out[o0 + j * 128:o0 + (j + 1) * 128, :], res[:, j, :],
                              reads=[("res", j)], writes=[("out", t, j)])
                S.barrier()
            if stop_after == 'C':
                raise _Stop
        except _Stop:
            S.barrier()
        nc._n_sched_inst = S.ninst
    return nc


_CACHE = {}


def _in_maps(inputs):
    x = np.asarray(inputs["x"], dtype=np.float32)
    mem = np.asarray(inputs["mem"], dtype=np.float32)
    positions = np.asarray(inputs["positions"], dtype=np.int32)
    half = 64
    invf = (np.float32(10000.0) ** (-np.arange(half, dtype=np.float32) / np.float32(half))).astype(np.float32)
    invf_col = np.ascontiguousarray(np.concatenate([invf, invf])[:, None])

    def rep(v):
        return np.ascontiguousarray(np.broadcast_to(np.asarray(v, np.float32).reshape(1, -1), (128, v.size)))

    shared = {"invf": invf_col}
    for nm in ("ln_in_g", "ln_in_b", "ln1_g", "ln1_b", "ln2_g", "ln2_b", "ln3_g", "ln3_b",
               "sgu_norm_g", "sgu_norm_b", "attn_out_g", "gmlp_out_g"):
        shared[nm] = rep(np.asarray(inputs[nm], np.float32).reshape(-1))
    shared["w_spatial"] = np.ascontiguousarray(np.asarray(inputs["w_spatial"], np.float32)[0])
    shared["b_spatial"] = np.ascontiguousarray(np.asarray(inputs["b_spatial"], np.float32)[0])
    names = {"w_in": "w_in", "w_mix": "w_mix_out", "w_xq": "w_xq", "w_xo": "w_xo", "w_gate": "w_ffn_gate",
             "w_up": "w_ffn_up", "w_down": "w_ffn_down", "w_xk": "w_xk", "w_xv": "w_xv"}
    for k, src in names.items():
        shared[k] = np.ascontiguousarray(np.asarray(inputs[src], np.float32)[0])
    maps = []
    for c in range(8):
        b, s = divmod(c, 4)
        lo = s * TOWN - THALO
        xc = np.zeros((TTOT, D), np.float32)
        pc = np.zeros((TTOT,), np.int32)
        if s == 0:
            xc[THALO:] = x[b, 0:TOWN]
            pc[THALO:] = positions[b, 0:TOWN]
            hvv = 0.0
        else:
            xc[:] = x[b, lo:lo + TTOT]
            pc[:] = positions[b, lo:lo + TTOT]
            hvv = 1.0
        m = dict(shared)
        m["x"] = xc
        m["mem"] = np.ascontiguousarray(mem[b])
        m["pos"] = np.ascontiguousarray(np.broadcast_to(pc[None, :], (128, TTOT)))
        m["hv"] = np.full((128, 1), hvv, np.float32)
        maps.append(m)
    return maps


def kernel(**inputs):
    if "nc" not in _CACHE:
        _CACHE["nc"] = build_program(debug=False)
    nc = _CACHE["nc"]
    maps = _in_maps(inputs)
    res = run_bass_kernel_spmd(nc, maps, core_ids=list(range(8)))
    outp = np.empty((2, 4 * TOWN, D), np.float32)
    for c in range(8):
        b, s = divmod(c, 4)
        outp[b, s * TOWN:(s + 1) * TOWN] = np.asarray(res.results[c]["out"], np.float32)
    return outp
```
